# Optimizing a Trainium2 kernel written in Bass

```python
import math
import jax, jax.numpy as jnp
from jax import lax
import numpy as np

D_MODEL = 1024
BATCH = 4
SEQ = 8192
DEPTH = 2

N_MIXERS = 2
A_CONFIGS = ((128, 1), (512, 4), (2048, 16))
A_GROUPS = len(A_CONFIGS)
A_HEADS = 16
A_HEAD_DIM = D_MODEL // A_HEADS
A_WIDTH = A_HEADS * A_HEAD_DIM
N_BUCKETS = 32
MAX_DISTANCE = 2048
B_HEADS = 4
B_DK = D_MODEL // 2 // B_HEADS
B_DV = D_MODEL // B_HEADS
B_QK = B_HEADS * B_DK
B_V = B_HEADS * B_DV
B_GATE_RANK = 16
B_TAU = 16.0
B_CHUNK = 64
B_IN = 2 * B_QK + B_V + B_GATE_RANK + B_V
D_FF = 2816
CONV_W = 3
EPS = 1e-6
NEG_INF = -1e30
N_A = (DEPTH + 1) // 2
N_B = DEPTH // 2

kernel_name = "hybrid_dilated_gla_convffn"


def rmsnorm(x, g):
    x32 = x.astype(jnp.float32)
    y = x32 * lax.rsqrt(jnp.mean(x32 * x32, axis=-1, keepdims=True) + EPS)
    return (y * g.astype(jnp.float32)).astype(x.dtype)


def t5_bucket(dist):
    max_exact = N_BUCKETS // 2
    n = jnp.maximum(dist, max_exact).astype(jnp.float32)
    large = max_exact + (jnp.log(n / max_exact) / math.log(MAX_DISTANCE / max_exact)
                         * (N_BUCKETS - max_exact)).astype(jnp.int32)
    large = jnp.minimum(large, N_BUCKETS - 1)
    return jnp.where(dist < max_exact, dist, large)


def dilated_branch(q, k, v, table, window, dilation):
    B, S, H, E = q.shape
    blk = window // dilation
    L = S // dilation
    nb = -(-L // blk)
    Lp = nb * blk

    def to_blocks(t):
        t = t.reshape(B, L, dilation, H, E)
        t = jnp.pad(t, ((0, 0), (0, Lp - L), (0, 0), (0, 0), (0, 0)))
        return t.reshape(B, nb, blk, dilation, H, E)

    def with_prev(t):
        prev = jnp.pad(t[:, :-1], ((0, 0), (1, 0), (0, 0), (0, 0), (0, 0), (0, 0)))
        return jnp.concatenate([prev, t], axis=2)

    qb = to_blocks(q)
    kk = with_prev(to_blocks(k))
    vv = with_prev(to_blocks(v))

    qi = jnp.arange(blk)[:, None]
    ki = jnp.arange(2 * blk)[None, :]
    steps = qi + blk - ki
    band = (steps >= 0) & (steps <= blk)
    first = (jnp.arange(nb)[:, None, None] == 0) & (ki[None] < blk)
    mask = band[None] & ~first
    bucket = t5_bucket(jnp.clip(steps, 0, blk) * dilation)
    bias = jnp.transpose(table[bucket], (2, 0, 1)).astype(jnp.float32)

    s = jnp.einsum('bnqrhe,bnkrhe->bnrhqk', qb, kk).astype(jnp.float32) + bias[None, None, None]
    s = jnp.where(mask[None, :, None, None], s, NEG_INF)
    m = jnp.max(s, axis=-1, keepdims=True)
    p = jnp.exp(s - m)
    den = jnp.sum(p, axis=-1)
    o = jnp.einsum('bnrhqk,bnkrhe->bnqrhe', p, vv.astype(jnp.float32))
    den_t = jnp.transpose(den, (0, 1, 4, 2, 3))
    o = o / den_t[..., None]
    lse = jnp.transpose(m[..., 0] + jnp.log(den), (0, 1, 4, 2, 3))
    o = o.reshape(B, Lp, dilation, H, E)[:, :L].reshape(B, S, H, E)
    lse = lse.reshape(B, Lp, dilation, H)[:, :L].reshape(B, S, H)
    return o, lse


def dilated_mixture(h, w_in, w_out, rel_bias):
    B, S, _ = h.shape
    qkv = (h @ w_in).reshape(B, S, A_GROUPS, 3, A_HEADS, A_HEAD_DIM)
    scale = A_HEAD_DIM ** -0.5
    outs, lses = [], []
    for g, (window, dilation) in enumerate(A_CONFIGS):
        o, lse = dilated_branch(qkv[:, :, g, 0] * scale, qkv[:, :, g, 1], qkv[:, :, g, 2],
                                rel_bias[:, g * A_HEADS:(g + 1) * A_HEADS], window, dilation)
        outs.append(o)
        lses.append(lse)
    wts = jax.nn.softmax(jnp.stack(lses), axis=0)
    o = jnp.sum(wts[..., None] * jnp.stack(outs), axis=0)
    return o.reshape(B, S, A_WIDTH).astype(h.dtype) @ w_out


def gla_mixer(h, w_in, w_gate, b_gate, g_norm, w_out):
    B, S, _ = h.shape
    N = S // B_CHUNK
    proj = h @ w_in
    o0 = 0
    q = proj[..., o0:o0 + B_QK]; o0 += B_QK
    k = proj[..., o0:o0 + B_QK]; o0 += B_QK
    v = proj[..., o0:o0 + B_V]; o0 += B_V
    glr = proj[..., o0:o0 + B_GATE_RANK]; o0 += B_GATE_RANK
    r = proj[..., o0:o0 + B_V]
    gk = jax.nn.log_sigmoid((glr @ w_gate + b_gate).astype(jnp.float32)) / B_TAU

    def chunks(t, e):
        t = t.astype(jnp.float32).reshape(B, N, B_CHUNK, B_HEADS, e)
        return jnp.transpose(t, (0, 3, 1, 2, 4))

    q, k, gk = chunks(q, B_DK) * (B_DK ** -0.5), chunks(k, B_DK), chunks(gk, B_DK)
    v = chunks(v, B_DV)
    bcum = jnp.cumsum(gk, axis=3)
    blast = bcum[:, :, :, -1:, :]
    q_t = q * jnp.exp(bcum)
    k_t = k * jnp.exp(-bcum)
    k_d = k * jnp.exp(blast - bcum)
    causal = jnp.tril(jnp.ones((B_CHUNK, B_CHUNK), dtype=bool))
    A = jnp.where(causal, jnp.einsum('bhncd,bhnsd->bhncs', q_t, k_t), 0.0)
    o_intra = jnp.einsum('bhncs,bhnse->bhnce', A, v)
    kv = jnp.einsum('bhncd,bhnce->bhnde', k_d, v)
    decay = jnp.exp(blast[:, :, :, 0, :])

    def step(state, inp):
        dec, inc = inp
        return dec[..., None] * state + inc, state

    _, s_prev = lax.scan(step, jnp.zeros((B, B_HEADS, B_DK, B_DV), jnp.float32),
                         (jnp.moveaxis(decay, 2, 0), jnp.moveaxis(kv, 2, 0)))
    s_prev = jnp.moveaxis(s_prev, 0, 2)
    o = o_intra + jnp.einsum('bhncd,bhnde->bhnce', q_t, s_prev)
    o = jnp.transpose(o, (0, 2, 3, 1, 4)).reshape(B, S, B_HEADS, B_DV)
    o = rmsnorm(o, g_norm).reshape(B, S, B_V).astype(h.dtype)
    return (o * jax.nn.silu(r)) @ w_out


def conv_ffn(h, w_up, conv_w, conv_b, w_down):
    u = h @ w_up
    up = jnp.pad(u, ((0, 0), (CONV_W - 1, 0), (0, 0)))
    S = h.shape[1]
    u = conv_b + sum(conv_w[j] * up[:, j:j + S] for j in range(CONV_W))
    a, b = jnp.split(u, 2, axis=-1)
    return (jax.nn.silu(a) * b) @ w_down


def setup_inputs(seed: int = 0) -> dict:
    key = jax.random.key(seed)
    ks = jax.random.split(key, 24)
    f32 = jnp.float32
    nrm = lambda k, shape, s: jax.random.normal(k, shape, f32) * s
    D = D_MODEL
    return {
        "x": nrm(ks[0], (BATCH, SEQ, D), 1.0),
        "c": nrm(ks[1], (BATCH, D), 1.0),
        "w_in_a": nrm(ks[2], (N_A, D, A_GROUPS * 3 * A_WIDTH), D ** -0.5),
        "w_out_a": nrm(ks[3], (N_A, A_WIDTH, D), A_WIDTH ** -0.5),
        "rel_bias": nrm(ks[4], (N_BUCKETS, A_GROUPS * A_HEADS), 0.5),
        "w_in_b": nrm(ks[5], (N_B, D, B_IN), D ** -0.5),
        "w_gate_b": nrm(ks[6], (N_B, B_GATE_RANK, B_QK), B_GATE_RANK ** -0.5),
        "b_gate_b": nrm(ks[7], (N_B, B_QK), 0.1),
        "gnorm_b": 1.0 + nrm(ks[8], (N_B, B_DV), 0.02),
        "w_out_b": nrm(ks[9], (N_B, B_V, D), B_V ** -0.5),
        "norm_mix": 1.0 + nrm(ks[10], (DEPTH, D), 0.02),
        "norm_ffn": 1.0 + nrm(ks[11], (DEPTH, D), 0.02),
        "w_ada": nrm(ks[12], (DEPTH, D, 6 * D), 0.5 * D ** -0.5),
        "b_ada": nrm(ks[13], (DEPTH, 6 * D), 0.02),
        "w_up": nrm(ks[14], (DEPTH, D, 2 * D_FF), D ** -0.5),
        "conv_w": nrm(ks[15], (DEPTH, CONV_W, 2 * D_FF), CONV_W ** -0.5),
        "conv_b": nrm(ks[16], (DEPTH, 2 * D_FF), 0.02),
        "w_down": nrm(ks[17], (DEPTH, D_FF, D), D_FF ** -0.5),
        "norm_final": 1.0 + nrm(ks[18], (D,), 0.02),
    }


def reference(x, c, w_in_a, w_out_a, rel_bias, w_in_b, w_gate_b, b_gate_b, gnorm_b, w_out_b,
              norm_mix, norm_ffn, w_ada, b_ada, w_up, conv_w, conv_b, w_down, norm_final):
    for i in range(DEPTH):
        mod = jax.nn.silu(c) @ w_ada[i] + b_ada[i]
        sh1, sc1, g1, sh2, sc2, g2 = [m[:, None, :] for m in jnp.split(mod, 6, axis=-1)]
        h = rmsnorm(x, norm_mix[i]) * (1.0 + sc1) + sh1
        if i % N_MIXERS == 0:
            j = i // N_MIXERS
            y = dilated_mixture(h, w_in_a[j], w_out_a[j], rel_bias)
        else:
            j = i // N_MIXERS
            y = gla_mixer(h, w_in_b[j], w_gate_b[j], b_gate_b[j], gnorm_b[j], w_out_b[j])
        x = x + g1 * y
        h = rmsnorm(x, norm_ffn[i]) * (1.0 + sc2) + sh2
        x = x + g2 * conv_ffn(h, w_up[i], conv_w[i], conv_b[i], w_down[i])
    return rmsnorm(x, norm_final)
```

```python
import math
import numpy as np
from contextlib import ExitStack, contextmanager
import concourse.bass as bass
import concourse.mybir as mybir
from concourse.bass_utils import run_bass_kernel_spmd

F32 = mybir.dt.float32
BF16 = mybir.dt.bfloat16
AF = mybir.ActivationFunctionType
ALU = mybir.AluOpType

ENGS = ("pe", "act", "dve", "pool", "sp")
EPS = 1e-6
A_CONFIGS = ((128, 1), (512, 4), (2048, 16))
D_FF = 2816


class Buf:
    def __init__(self, t, name):
        self.t = t
        self.name = name
        self.w = {}
        self.r = {}
        self.dsem = None

    def __getitem__(self, idx):
        return self.t[idx]


class Pool:
    def __init__(self, bufs):
        self.bufs = bufs
        self.i = 0

    def next(self):
        b = self.bufs[self.i % len(self.bufs)]
        self.i += 1
        return b


class FW:
    def __init__(self, nc, es, dbg=()):
        self.nc = nc
        self.es = es
        self.cur = es
        self.ops = {e: [] for e in ENGS}
        self.cnt = {e: 0 for e in ENGS}
        self.seen = {e: {} for e in ENGS}
        self.sems = {}
        self.dcount = {}
        self.free_dsems = []
        self.phase_bufs = None
        self.dbg = set(dbg)
        self.dbg_out = []
        for e in ENGS:
            self.sems[("eng", e)] = es.enter_context(nc.semaphore("sem_" + e))
        self.nbuf = 0
        self.ndsem = 0

    def sb(self, shape, dt, name=None):
        self.nbuf += 1
        name = (name or "sb") + f"_{self.nbuf}"
        t = self.cur.enter_context(self.nc.sbuf_tensor(name, list(shape), dt))
        b = Buf(t, name)
        if self.phase_bufs is not None:
            self.phase_bufs.append(b)
        return b

    def ps(self, shape, dt=F32, name=None):
        self.nbuf += 1
        name = (name or "ps") + f"_{self.nbuf}"
        t = self.es.enter_context(self.nc.psum_tensor(name, list(shape), dt))
        return Buf(t, name)

    def pool(self, n, shape, dt, name=None):
        return Pool([self.sb(shape, dt, name) for _ in range(n)])

    def dram(self, name, shape, dt):
        kind = "Internal"
        if name in self.dbg:
            kind = "ExternalOutput"
            self.dbg_out.append(name)
        t = self.nc.dram_tensor(name, list(shape), dt, kind=kind)
        return Buf(t, name)

    def ext(self, name, shape, dt, kind):
        t = self.nc.dram_tensor(name, list(shape), dt, kind=kind)
        return Buf(t, name)

    def _dsem(self, b):
        if b.dsem is None:
            if self.free_dsems:
                b.dsem = self.free_dsems.pop()
            else:
                self.ndsem += 1
                key = ("dma", self.ndsem)
                self.sems[key] = self.es.enter_context(self.nc.semaphore(f"dsem{self.ndsem}"))
                self.dcount[key] = 0
                b.dsem = key
        return b.dsem

    @contextmanager
    def phase(self):
        with ExitStack() as pes:
            self.cur = pes
            self.phase_bufs = []
            yield
            self.barrier()
            for b in self.phase_bufs:
                if b.dsem is not None:
                    self.free_dsems.append(b.dsem)
            self.phase_bufs = None
            self.cur = self.es

    def barrier(self):
        for e in ENGS:
            need = {}
            for e2 in ENGS:
                if e2 != e and self.cnt[e2] > 0:
                    need[("eng", e2)] = self.cnt[e2]
            for k, v in self.dcount.items():
                if v > 0:
                    need[k] = v
            waits = []
            seen = self.seen[e]
            for k, v in need.items():
                if seen.get(k, 0) >= v:
                    continue
                seen[k] = v
                waits.append((k, v))
            self.ops[e].append((waits, None, None, 0))

    def _waits(self, eng, reads, writes, part):
        need = {}

        def add(evs):
            for k, v in evs.items():
                if need.get(k, 0) < v:
                    need[k] = v
        for b in reads:
            add(b.w)
        for b in writes:
            add(b.r)
            add(b.w)
        out = []
        seen = self.seen[eng]
        for k, v in need.items():
            if eng == "pe" and k == ("eng", "pe"):
                continue
            if seen.get(k, 0) >= v:
                continue
            seen[k] = v
            out.append((k, v))
        return out

    @staticmethod
    def _commit(ev, reads, writes, part):
        k, v = ev
        for b in reads:
            if b.r.get(k, 0) < v:
                b.r[k] = v
        for b in writes:
            if part:
                if b.w.get(k, 0) < v:
                    b.w[k] = v
            else:
                b.w = {k: v}
                b.r = {}

    def op(self, eng, fn, reads=(), writes=(), part=False):
        waits = self._waits(eng, reads, writes, part)
        self.cnt[eng] += 1
        ev = (("eng", eng), self.cnt[eng])
        self.ops[eng].append((waits, fn, ev[0], 1))
        self._commit(ev, reads, writes, part)

    def mm(self, fn, reads=(), writes=(), last=True):
        waits = self._waits("pe", reads, writes, True)
        if last:
            self.cnt["pe"] += 1
            ev = (("eng", "pe"), self.cnt["pe"])
            self.ops["pe"].append((waits, fn, ev[0], 1))
        else:
            ev = (("eng", "pe"), self.cnt["pe"] + 1)
            self.ops["pe"].append((waits, fn, None, 0))
        self._commit(ev, reads, writes, True)

    def dma(self, out_b, out_ap, in_b, in_ap, part=False, q="sp"):
        primary = out_b if isinstance(out_b.t, bass.SBTensorHandle) else in_b
        key = self._dsem(primary)
        waits = self._waits(q, [in_b], [out_b], part)
        self.dcount[key] += 16
        ev = (key, self.dcount[key])

        def fn(e, out_ap=out_ap, in_ap=in_ap):
            return e.dma_start(out=out_ap, in_=in_ap)
        self.ops[q].append((waits, fn, key, 16))
        self._commit(ev, [in_b], [out_b], part)

    def final_wait(self, eng, bufs):
        need = {}
        for b in bufs:
            for d in (b.w, b.r):
                for k, v in d.items():
                    if need.get(k, 0) < v:
                        need[k] = v
        self.ops[eng].append((list(need.items()), None, None, 0))

    def act(self, out_b, out_ap, in_b, in_ap, func, scale=None, bias=None, reads=(), part=True):
        kw = {}
        if scale is not None:
            kw["scale"] = scale
        if bias is not None:
            kw["bias"] = bias
        self.op("act", lambda e: e.activation(out=out_ap, in_=in_ap, func=func, **kw),
                reads=[in_b, *reads], writes=[out_b], part=part)

    def tt(self, out_b, out_ap, a_b, a_ap, b_b, b_ap, op, eng="dve", part=True):
        self.op(eng, lambda e: e.tensor_tensor(out=out_ap, in0=a_ap, in1=b_ap, op=op),
                reads=[a_b, b_b], writes=[out_b], part=part)

    def stt(self, out_b, out_ap, a_b, a_ap, scalar, b_b, b_ap, op0, op1, reads=(), part=True):
        self.op("dve", lambda e: e.scalar_tensor_tensor(out=out_ap, in0=a_ap, scalar=scalar, in1=b_ap,
                                                         op0=op0, op1=op1),
                reads=[a_b, b_b, *reads], writes=[out_b], part=part)

    def copy(self, eng, out_b, out_ap, in_b, in_ap, part=True):
        if eng == "act":
            self.op("act", lambda e: e.activation(out=out_ap, in_=in_ap, func=AF.Copy),
                    reads=[in_b], writes=[out_b], part=part)
        else:
            self.op(eng, lambda e: e.tensor_copy(out=out_ap, in_=in_ap), reads=[in_b], writes=[out_b], part=part)

    def memset(self, eng, b, ap, val, part=True):
        self.op(eng, lambda e: e.memset(ap, val), writes=[b], part=part)

    def emit(self):
        nc = self.nc
        sems = self.sems
        ops = self.ops
        with nc.Block() as block:
            def run(e, lst):
                for waits, fn, inc, n in lst:
                    for k, v in waits:
                        e.wait_ge(sems[k], v)
                    if fn is None:
                        continue
                    inst = fn(e)
                    if inc is not None:
                        inst.then_inc(sems[inc], n)

            @block.tensor
            def _(e):
                run(e, ops["pe"])

            @block.scalar
            def _(e):
                run(e, ops["act"])

            @block.vector
            def _(e):
                run(e, ops["dve"])

            @block.gpsimd
            def _(e):
                run(e, ops["pool"])

            @block.sync
            def _(e):
                run(e, ops["sp"])


class Prog:
    def __init__(self, NT, dbg=(), upto=99):
        self.NT = NT
        self.upto = upto
        nc = bass.Bass("TRN2", target_bir_lowering=False)
        self.nc = nc
        with ExitStack() as es:
            fw = FW(nc, es, dbg)
            self.fw = fw
            self.declare_io()
            self.consts()
            self.run()
            fw.emit()

    def kview(self, D, t0, n):
        return D.t.ap().rearrange("(k p) t -> p k t", p=128)[:, :, t0:t0 + n]

    def declare_io(self):
        fw, NT = self.fw, self.NT
        I = {}

        def inp(name, shape):
            I[name] = fw.ext(name, shape, F32, "ExternalInput")
        inp("xT", [1024, NT])
        inp("cT", [128, 8])
        inp("w_ada", [2, 6, 128, 8, 1024])
        inp("b_ada", [2, 128, 48])
        inp("nmix", [2, 128, 8])
        inp("nffn", [2, 128, 8])
        inp("nfin", [128, 8])
        inp("w_in_a", [18, 128, 8, 512])
        inp("w_out_a", [128, 8, 1024])
        inp("tbias", [3, 8, 128, 512])
        inp("w_in_b", [6, 128, 8, 512])
        inp("w_glr", [128, 8, 16])
        inp("w_gate", [16, 512])
        inp("b_gate", [128, 4])
        inp("gnorm", [128, 2])
        inp("w_out_b", [128, 8, 1024])
        inp("w_up", [2, 22, 128, 2, 8, 128])
        inp("cw", [2, 128, 44, 4])
        inp("w_down", [2, 22, 128, 1024])
        inp("maskc", [128, 512])
        inp("bmask", [128, 128])
        inp("identf", [128, 128])
        self.I = I
        self.OUT = fw.ext("yT", [1024, NT], F32, "ExternalOutput")
        S = {}
        for g in range(3):
            S[f"QT{g}"] = fw.dram(f"QT{g}", [1024, NT], BF16)
            S[f"KT{g}"] = fw.dram(f"KT{g}", [1024, NT], BF16)
            S[f"V{g}"] = fw.dram(f"V{g}", [NT, 16, 128], BF16)
        S["OT"] = fw.dram("OT", [1024, NT], BF16)
        S["H2"] = fw.dram("H2", [1024, NT], BF16)
        for i in range(1, 5):
            S[f"X{i}"] = fw.dram(f"X{i}", [1024, NT], F32)
        S["Q1T"] = fw.dram("Q1T", [512, NT], F32)
        S["K1T"] = fw.dram("K1T", [512, NT], F32)
        S["VB"] = fw.dram("VB", [NT, 1024], BF16)
        S["R1T"] = fw.dram("R1T", [1024, NT], F32)
        S["G1T"] = fw.dram("G1T", [16, NT], BF16)
        self.S = S

    def consts(self):
        fw, I = self.fw, self.I
        self.pp = Pool([fw.ps([128, 512], F32) for _ in range(7)])
        self.pst = fw.ps([128, 1024], BF16)
        self.ones_bf = fw.sb([128, 128], BF16, "ones")
        fw.memset("dve", self.ones_bf, self.ones_bf[:, :], 1.0, part=False)
        self.onesz = fw.sb([128, 2, 128], BF16, "onesz")
        fw.memset("dve", self.onesz, self.onesz[:, :, :], 0.0, part=False)
        fw.memset("dve", self.onesz, self.onesz[:, 0, 0:64], 1.0)
        fw.memset("dve", self.onesz, self.onesz[:, 1, 64:128], 1.0)
        idf = fw.sb([128, 128], F32, "identf")
        fw.dma(idf, idf[:, :], I["identf"], I["identf"][:, :])
        self.ident = fw.sb([128, 128], BF16, "ident")
        fw.copy("dve", self.ident, self.ident[:, :], idf, idf[:, :], part=False)
        self.modt = [fw.sb([128, 48], F32, f"modt{i}") for i in range(2)]
        self.Amod = [fw.sb([128, 16], F32, f"amod{i}") for i in range(2)]
        self.nfin = fw.sb([128, 8], F32, "nfin")
        fw.dma(self.nfin, self.nfin[:, :], I["nfin"], I["nfin"][:, :])

    def run(self):
        S, I = self.S, self.I
        up = self.upto
        self.phase_mod()
        if up >= 1:
            self.phase_A0()
        if up >= 2:
            self.phase_B0()
        if up >= 3:
            self.phase_C1(0, S["OT"], I["w_out_a"], I["xT"], S["X1"])
        if up >= 4:
            self.phase_C2(0, S["X1"], S["X2"])
        if up >= 5:
            self.phase_A1()
        if up >= 6:
            self.phase_B1()
        if up >= 7:
            self.phase_C1(1, S["OT"], I["w_out_b"], S["X2"], S["X3"])
        if up >= 8:
            self.phase_C2(1, S["X3"], S["X4"])
        if up >= 9:
            self.phase_F(S["X4"])
            self.fw.final_wait("sp", [self.OUT])
        else:
            self.fw.barrier()

    def phase_mod(self):
        fw, I, pp = self.fw, self.I, self.pp
        with fw.phase():
            ct = fw.sb([128, 8], F32)
            fw.dma(ct, ct[:, :], I["cT"], I["cT"][:, :])
            sc = fw.sb([128, 8], F32)
            fw.act(sc, sc[:, :], ct, ct[:, :], AF.Silu, part=False)
            wpool = fw.pool(2, [128, 8, 1024], F32, "wada")
            for i in range(2):
                psm = pp.next()
                for blk in range(6):
                    wt = wpool.next()
                    fw.dma(wt, wt[:, :, :], I["w_ada"], I["w_ada"][i, blk])
                    for fc in range(8):
                        col = blk * 8 + fc
                        for k in range(8):
                            fw.mm(lambda e, wt=wt, k=k, fc=fc, col=col, psm=psm: e.matmul(
                                psm[:, col:col + 1], wt[:, k, fc * 128:(fc + 1) * 128], sc[:, k:k + 1],
                                start=(k == 0), stop=(k == 7)),
                                reads=[wt, sc], writes=[psm], last=(k == 7))
                bt = fw.sb([128, 48], F32)
                fw.dma(bt, bt[:, :], I["b_ada"], I["b_ada"][i])
                nm = fw.sb([128, 16], F32)
                fw.dma(nm, nm[:, 0:8], I["nmix"], I["nmix"][i])
                fw.dma(nm, nm[:, 8:16], I["nffn"], I["nffn"][i], part=True)
                mt = self.modt[i]
                fw.tt(mt, mt[:, :], psm, psm[:, 0:48], bt, bt[:, :], ALU.add, part=False)
                am = self.Amod[i]
                fw.stt(am, am[:, 0:8], mt, mt[:, 8:16], 1.0, nm, nm[:, 0:8], ALU.add, ALU.mult, part=False)
                fw.stt(am, am[:, 8:16], mt, mt[:, 32:40], 1.0, nm, nm[:, 8:16], ALU.add, ALU.mult)

    def A1(self, i, k): return self.Amod[i], self.Amod[i][:, k:k + 1]
    def A2(self, i, k): return self.Amod[i], self.Amod[i][:, 8 + k:9 + k]
    def B1(self, i, k): return self.modt[i], self.modt[i][:, k:k + 1]
    def G1(self, i, k): return self.modt[i], self.modt[i][:, 16 + k:17 + k]
    def B2(self, i, k): return self.modt[i], self.modt[i][:, 24 + k:25 + k]
    def G2(self, i, k): return self.modt[i], self.modt[i][:, 40 + k:41 + k]

    def mk_norm_pools(self):
        fw = self.fw
        self.sqp = fw.pool(2, [128, 8, 512], BF16, "sq")
        self.stdp = fw.pool(2, [128, 512], F32, "std")
        self.rstdp = fw.pool(2, [128, 512], F32, "rstd")
        self.tmpp = fw.pool(3, [128, 512], F32, "ntmp")

    def norm_mod(self, xt, Af, Bf, out_b, out_ap):
        fw = self.fw
        sq = self.sqp.next()
        fw.act(sq, sq[:, :, :], xt, xt[:, :, :], AF.Square, part=False)
        pss = self.pp.next()
        for k in range(8):
            fw.mm(lambda e, k=k, sq=sq, pss=pss: e.matmul(pss[:, :], self.ones_bf[:, :], sq[:, k, :],
                                                           start=(k == 0), stop=(k == 7)),
                  reads=[sq, self.ones_bf], writes=[pss], last=(k == 7))
        std = self.stdp.next()
        fw.act(std, std[:, :], pss, pss[:, :], AF.Ln, scale=1.0 / 1024, bias=EPS, part=False)
        rstd = self.rstdp.next()
        fw.act(rstd, rstd[:, :], std, std[:, :], AF.Exp, scale=-0.5, part=False)
        for k in range(8):
            tmp = self.tmpp.next()
            ab, aap = Af(k)
            bb, bap = Bf(k)
            fw.stt(tmp, tmp[:, :], xt, xt[:, k, :], aap, rstd, rstd[:, :], ALU.mult, ALU.mult,
                   reads=[ab], part=False)
            fw.act(out_b, out_ap(k), tmp, tmp[:, :], AF.Identity, bias=bap, reads=[bb], part=True)

    def load_w_bf(self, fpool, bpool, src_b, src_ap, shape_ap):
        fw = self.fw
        wf = fpool.next()
        fw.dma(wf, shape_ap(wf), src_b, src_ap)
        wb = bpool.next()
        fw.copy("pool", wb, shape_ap(wb), wf, shape_ap(wf), part=False)
        return wb

    def proj_fm(self, wb, fc, hT, tt, evac):
        fw = self.fw
        ps = self.pp.next()
        for k in range(8):
            fw.mm(lambda e, k=k, ps=ps: e.matmul(ps[:, :], wb[:, k, fc * 128:(fc + 1) * 128],
                                                  hT[:, k, tt * 512:(tt + 1) * 512],
                                                  start=(k == 0), stop=(k == 7)),
                  reads=[wb, hT], writes=[ps], last=(k == 7))
        evac(ps)

    def phase_A0(self):
        fw, I, S, NT = self.fw, self.I, self.S, self.NT
        with fw.phase():
            self.mk_norm_pools()
            hT = fw.sb([128, 8, 2048], BF16, "hT")
            xp = fw.pool(2, [128, 8, 512], F32, "xt")
            wfp = fw.pool(2, [128, 8, 512], F32, "wf")
            wbp = fw.pool(2, [128, 8, 512], BF16, "wb")
            stgp = fw.pool(3, [128, 2048], BF16, "stg")
            vstp = fw.pool(4, [128, 8, 128], BF16, "vst")
            for b in vstp.bufs:
                fw.memset("pool", b, b[:, :, :], 1.0, part=False)
            full3 = lambda b: b[:, :, :]
            ev = [0]
            for st in range(NT // 2048):
                for tt in range(4):
                    xt = xp.next()
                    t0 = st * 2048 + tt * 512
                    fw.dma(xt, xt[:, :, :], I["xT"], self.kview(I["xT"], t0, 512))
                    self.norm_mod(xt, lambda k: self.A1(0, k), lambda k: self.B1(0, k), hT,
                                  lambda k, tt=tt: hT[:, k, tt * 512:(tt + 1) * 512])
                nxt = self.load_w_bf(wfp, wbp, I["w_in_a"], I["w_in_a"][0], full3)
                for blk in range(18):
                    wb = nxt
                    if blk + 1 < 18:
                        nxt = self.load_w_bf(wfp, wbp, I["w_in_a"], I["w_in_a"][blk + 1], full3)
                    g, j, half = blk // 6, (blk % 6) // 2, blk % 2
                    if j < 2:
                        dst = S[("QT" if j == 0 else "KT") + str(g)]
                        for fc in range(4):
                            stg = stgp.next()
                            for tt in range(4):
                                def evac(ps, stg=stg, tt=tt, j=j):
                                    ev[0] += 1
                                    oap = stg[:, tt * 512:(tt + 1) * 512]
                                    if j == 0:
                                        fw.act(stg, oap, ps, ps[:, :], AF.Copy, scale=0.125, part=(tt > 0))
                                    elif ev[0] % 2 == 0:
                                        fw.copy("dve", stg, oap, ps, ps[:, :], part=(tt > 0))
                                    else:
                                        fw.copy("act", stg, oap, ps, ps[:, :], part=(tt > 0))
                                self.proj_fm(wb, fc, hT, tt, evac)
                            r0 = half * 512 + fc * 128
                            fw.dma(dst, dst[r0:r0 + 128, st * 2048:(st + 1) * 2048], stg, stg[:, :], part=True)
                    else:
                        dst = S[f"V{g}"]
                        for tb in range(16):
                            ps = self.pp.next()
                            for k in range(8):
                                fw.mm(lambda e, k=k, ps=ps, tb=tb, wb=wb: e.matmul(
                                    ps[:, :], hT[:, k, tb * 128:(tb + 1) * 128], wb[:, k, :],
                                    start=(k == 0), stop=(k == 7)),
                                    reads=[wb, hT], writes=[ps], last=(k == 7))
                            vst = vstp.next()
                            psv = ps[:, :].rearrange("p (h e) -> p h e", e=64)
                            fw.copy("act" if tb % 2 == 0 else "dve", vst, vst[:, :, 0:64], ps, psv[:, :, :],
                                    part=True)
                            tok0 = st * 2048 + tb * 128
                            fw.dma(dst, dst[tok0:tok0 + 128, half * 8:(half + 1) * 8, :], vst, vst[:, :, :],
                                   part=True)

    def phase_B0(self):
        fw, I, S, NT = self.fw, self.I, self.S, self.NT
        NTL = NT // 2048
        LOOK = 2
        with fw.phase():
            accH = [fw.sb([128, NT], F32, "accH0"), fw.sb([128, NT], F32, "accH1")]
            tbp = fw.pool(2, [128, 512], F32, "tb")
            ebp = fw.pool(2, [128, 512], F32, "eB")
            kp = fw.pool(4, [128, 2048], BF16, "Kt")
            qp = fw.pool(3, [128, 2, 2048], BF16, "Qt")
            for b in qp.bufs:
                fw.memset("pool", b, b[:, :, :], 0.0, part=False)
            vp = fw.pool(4, [128, 16, 2, 128], BF16, "Vt")
            ptp = fw.pool(4, [128, 512], F32, "Pt")
            pbp = fw.pool(6, [128, 512], BF16, "Pb")
            obp = fw.pool(2, [128, 2048], BF16, "ob")
            lnp = fw.pool(2, [128, 2048], F32, "lnd")
            onesz = self.onesz
            import os
            _cs = [int(v) for v in os.environ.get("B0_C", "0,1,2,3,4,5,6,7").split(",") if v != "none"]
            _gs = [int(v) for v in os.environ.get("B0_G", "0,1,2").split(",")]
            nblk = [0]
            for c in _cs:
                for g in _gs:
                    d = A_CONFIGS[g][1]
                    nbq = 16 // d
                    tb = tbp.next()
                    fw.dma(tb, tb[:, :], I["tbias"], I["tbias"][g, c])
                    eB = ebp.next()
                    fw.act(eB, eB[:, :], tb, tb[:, :], AF.Exp, part=False)
                    KT, QT, V = S[f"KT{g}"], S[f"QT{g}"], S[f"V{g}"]
                    Vv = V.t.ap().rearrange("(t bq i r) h e -> t r i bq (h e)", bq=nbq, i=128, r=d)

                    def load_tile(t):
                        Kt, Qt, Vt = kp.next(), qp.next(), vp.next()
                        cs = slice(t * 2048, (t + 1) * 2048)
                        fw.dma(Kt, Kt[:, :], KT, KT[c * 128:(c + 1) * 128, cs])
                        fw.dma(Qt, Qt[0:64, 0, :], QT, QT[c * 128:c * 128 + 64, cs], part=True)
                        fw.dma(Qt, Qt[64:128, 1, :], QT, QT[c * 128 + 64:(c + 1) * 128, cs], part=True)
                        Vtv = Vt[:, :, :, :].rearrange("p j h e -> p j (h e)")
                        for r in range(d):
                            fw.dma(Vt, Vtv[:, r * nbq:(r + 1) * nbq, :], V,
                                   Vv[t, r, :, :, c * 256:(c + 1) * 256], part=(r > 0))
                        return Kt, Qt, Vt
                    tiles = {0: load_tile(0)}
                    blocks = [(t, j) for t in range(NTL) for j in range(16)]
                    st = {}
                    grp = {}

                    def qk_stage(i):
                        t, j = blocks[i]
                        if j == 0 and t + 1 < NTL:
                            tiles[t + 1] = load_tile(t + 1)
                        Kt, Qt, Vt = tiles[t]
                        r, bq = divmod(j, nbq)
                        s0 = r + bq * 128 * d
                        cols = slice(s0, s0 + 127 * d + 1, d)
                        if bq > 0:
                            pK, pV, pj, pcols = Kt, Vt, j - 1, slice(s0 - 128 * d, s0 - d + 1, d)
                        elif t > 0:
                            ps0 = r + (nbq - 1) * 128 * d
                            pK, pV = tiles[t - 1][0], tiles[t - 1][2]
                            pj, pcols = r * nbq + nbq - 1, slice(ps0, ps0 + 127 * d + 1, d)
                        else:
                            pK = pV = pj = pcols = None
                        psS = self.pp.next()
                        for hh in range(2):
                            if pK is not None:
                                fw.mm(lambda e, psS=psS, hh=hh, pK=pK, pcols=pcols, Qt=Qt, cols=cols:
                                      e.matmul(psS[:, (hh * 2) * 128:(hh * 2 + 1) * 128], pK[:, pcols],
                                               Qt[:, hh, cols], start=True, stop=True),
                                      reads=[pK, Qt], writes=[psS], last=False)
                            fw.mm(lambda e, psS=psS, hh=hh, Kt=Kt, Qt=Qt, cols=cols:
                                  e.matmul(psS[:, (hh * 2 + 1) * 128:(hh * 2 + 2) * 128], Kt[:, cols],
                                           Qt[:, hh, cols], start=True, stop=True),
                                  reads=[Kt, Qt], writes=[psS], last=(hh == 1))
                        Pt = ptp.next()
                        if pK is None:
                            Ptv = Pt[:, :].rearrange("p (h c q) -> p h c q", h=2, c=2)
                            psv = psS[:, :].rearrange("p (h c q) -> p h c q", h=2, c=2)
                            fw.memset("pool", Pt, Ptv[:, :, 0, :], 0.0, part=False)
                            fw.act(Pt, Ptv[:, :, 1, :], psS, psv[:, :, 1, :], AF.Exp, part=True)
                        else:
                            fw.act(Pt, Pt[:, :], psS, psS[:, :], AF.Exp, part=False)
                        Pb = pbp.next()
                        nblk[0] += 1
                        meng = "pool" if nblk[0] % 4 == 0 else "dve"
                        fw.tt(Pb, Pb[:, :], Pt, Pt[:, :], eB, eB[:, :], ALU.mult, eng=meng, part=False)
                        st[i] = (Pb, pV, pj, Vt, pK is not None)

                    def pv_stage(i):
                        t, j = blocks[i]
                        Pb, pV, pj, Vt, hasp = st.pop(i)
                        jj = j % 4
                        j0 = j - jj
                        if jj == 0:
                            grp[0] = (self.pp.next(), self.pp.next())
                        pcs = [0, 1] if hasp else [1]
                        for hh in range(2):
                            psX = grp[0][hh]
                            for ci, pc in enumerate(pcs):
                                vb = pV if pc == 0 else Vt
                                vj = pj if pc == 0 else j
                                fw.mm(lambda e, psX=psX, jj=jj, vb=vb, vj=vj, Pb=Pb, hh=hh, pc=pc, ci=ci, n=len(pcs):
                                      e.matmul(psX[:, jj * 128:(jj + 1) * 128], vb[:, vj, hh, :],
                                               Pb[:, (hh * 2 + pc) * 128:(hh * 2 + pc + 1) * 128],
                                               start=(ci == 0), stop=(ci == n - 1)),
                                      reads=[vb, Pb], writes=[psX], last=(ci == len(pcs) - 1))
                        if jj == 3:
                            cs = slice(t * 2048, (t + 1) * 2048)
                            for hh in range(2):
                                acc, psX = accH[hh], grp[0][hh]
                                av = acc[:, cs].rearrange("p (bq i r) -> p r bq i", r=d, i=128)
                                if d == 16:
                                    av = av[:, j0:j0 + 4, 0, :]
                                elif d == 4:
                                    av = av[:, j0 // 4, :, :]
                                else:
                                    av = av[:, 0, j0:j0 + 4, :]
                                pv = psX[:, :].rearrange("p (a b) -> p a b", b=128)
                                if g == _gs[0]:
                                    fw.copy("act", acc, av, psX, pv, part=True)
                                else:
                                    fw.tt(acc, av, psX, pv, acc, av, ALU.add, part=True)
                    n = len(blocks)
                    for i in range(n + LOOK):
                        if i < n:
                            qk_stage(i)
                        if i - LOOK >= 0:
                            pv_stage(i - LOOK)
                OT = S["OT"]
                for t in range(NTL):
                    for hh in range(2):
                        acc = accH[hh]
                        ln = lnp.next()
                        ob = obp.next()
                        fw.act(ln, ln[64:128, :], acc, acc[64:128, t * 2048:(t + 1) * 2048], AF.Ln, part=False)
                        for qd in range(4):
                            ps = self.pp.next()
                            c0 = t * 2048 + qd * 512
                            fw.act(ps, ps[64:128, :], ln, ln[64:128, qd * 512:(qd + 1) * 512], AF.Exp, scale=-1.0,
                                   part=False)
                            fw.tt(ob, ob[0:64, qd * 512:(qd + 1) * 512], acc, acc[0:64, c0:c0 + 512],
                                  ps, ps[64:128, :], ALU.mult, part=(qd > 0))
                        r0 = c * 128 + hh * 64
                        fw.dma(OT, OT[r0:r0 + 64, t * 2048:(t + 1) * 2048], ob, ob[0:64, :], part=True)

    def phase_C1(self, li, OTb, WO, Xin, Xout):
        fw, S, NT = self.fw, self.S, self.NT
        with fw.phase():
            self.mk_norm_pools()
            wof = fw.sb([128, 8, 1024], F32, "wof")
            fw.dma(wof, wof[:, :, :], WO, WO[:, :, :])
            wo = fw.sb([128, 8, 1024], BF16, "wo")
            fw.copy("pool", wo, wo[:, :, :], wof, wof[:, :, :], part=False)
            otp = fw.pool(2, [128, 8, 512], BF16, "ot")
            xp = fw.pool(2, [128, 8, 512], F32, "xt")
            x1p = fw.pool(2, [128, 8, 512], F32, "x1")
            h2p = fw.pool(2, [128, 8, 512], BF16, "h2")
            H2 = S["H2"]

            def load(tt):
                ot, xt = otp.next(), xp.next()
                fw.dma(ot, ot[:, :, :], OTb, self.kview(OTb, tt * 512, 512))
                fw.dma(xt, xt[:, :, :], Xin, self.kview(Xin, tt * 512, 512))
                return ot, xt
            nxt = load(0)
            for tt in range(NT // 512):
                ot, xt = nxt
                if tt + 1 < NT // 512:
                    nxt = load(tt + 1)
                x1 = x1p.next()
                for dc in range(8):
                    ps = self.pp.next()
                    for k in range(8):
                        fw.mm(lambda e, k=k, ps=ps, dc=dc, ot=ot: e.matmul(
                            ps[:, :], wo[:, k, dc * 128:(dc + 1) * 128], ot[:, k, :],
                            start=(k == 0), stop=(k == 7)), reads=[wo, ot], writes=[ps], last=(k == 7))
                    gb, gap = self.G1(li, dc)
                    fw.stt(x1, x1[:, dc, :], ps, ps[:, :], gap, xt, xt[:, dc, :], ALU.mult, ALU.add,
                           reads=[gb], part=(dc > 0))
                fw.dma(Xout, self.kview(Xout, tt * 512, 512), x1, x1[:, :, :], part=True)
                h2 = h2p.next()
                first = [True]

                def oap(k, h2=h2):
                    return h2[:, k, :]
                self.norm_mod(x1, lambda k: self.A2(li, k), lambda k: self.B2(li, k), h2, oap)
                fw.dma(H2, self.kview(H2, tt * 512, 512), h2, h2[:, :, :], part=True)

    def phase_C2(self, li, Xin, Xout):
        fw, I, S, NT = self.fw, self.I, self.S, self.NT
        H2 = S["H2"]
        ST = 1024
        with fw.phase():
            wd = fw.sb([128, 22, 1024], BF16, "wd")
            wdf = fw.pool(2, [128, 1024], F32, "wdf")
            for m in range(22):
                wf = wdf.next()
                fw.dma(wf, wf[:, :], I["w_down"], I["w_down"][li, m])
                fw.copy("pool", wd, wd[:, m, :], wf, wf[:, :], part=(m > 0))
            cw = fw.sb([128, 44, 4], F32, "cw")
            fw.dma(cw, cw[:, :, :], I["cw"], I["cw"][li])
            Hh = fw.sb([128, 44, 2], F32, "halo")
            fw.memset("pool", Hh, Hh[:, :, :], 0.0, part=False)
            h2p = fw.pool(1, [128, 8, ST], BF16, "h2T")
            mT = fw.sb([128, 22, ST], BF16, "mT")
            wufp = fw.pool(2, [128, 2, 8, 128], F32, "wuf")
            wubp = fw.pool(2, [128, 2, 8, 128], BF16, "wub")
            up = fw.pool(4, [128, 514], F32, "u")
            tp = fw.pool(6, [128, 512], F32, "t")
            sap = fw.pool(2, [128, 512], F32, "sa")
            x1p = fw.pool(4, [128, 512], F32, "x1c")
            x2p = fw.pool(4, [128, 512], F32, "x2c")
            full4 = lambda b: b[:, :, :, :]
            for st in range(NT // ST):
                h2T = h2p.next()
                fw.dma(h2T, h2T[:, :, :], H2, self.kview(H2, st * ST, ST))
                def load_wu(j):
                    wf = wufp.next()
                    fw.dma(wf, wf[:, :, :, :], I["w_up"], I["w_up"][li, j])
                    wb = wubp.next()
                    fw.copy("act", wb, wb[:, 0, :, :], wf, wf[:, 0, :, :], part=False)
                    fw.copy("pool", wb, wb[:, 1, :, :], wf, wf[:, 1, :, :], part=True)
                    return wb
                nxt = load_wu(0)
                for j in range(22):
                    wu = nxt
                    if j + 1 < 22:
                        nxt = load_wu(j + 1)
                    for tt in range(ST // 512):
                        t3 = []
                        for ab in range(2):
                            ps = self.pp.next()
                            for k in range(8):
                                fw.mm(lambda e, k=k, ps=ps, ab=ab, wu=wu, tt=tt, h2T=h2T: e.matmul(
                                    ps[:, :], wu[:, ab, k, :], h2T[:, k, tt * 512:(tt + 1) * 512],
                                    start=(k == 0), stop=(k == 7)), reads=[wu, h2T], writes=[ps], last=(k == 7))
                            ch = ab * 22 + j
                            u = up.next()
                            fw.copy("pool", u, u[:, 0:2], Hh, Hh[:, ch, :], part=False)
                            fw.copy("act", u, u[:, 2:514], ps, ps[:, :], part=True)
                            fw.copy("pool", Hh, Hh[:, ch, :], u, u[:, 512:514], part=True)
                            t1 = tp.next()
                            fw.act(t1, t1[:, :], ps, ps[:, :], AF.Identity, scale=cw[:, ch, 2:3], bias=cw[:, ch, 3:4],
                                   reads=[cw], part=False)
                            t2 = tp.next()
                            fw.stt(t2, t2[:, :], u, u[:, 1:513], cw[:, ch, 1:2], t1, t1[:, :], ALU.mult, ALU.add,
                                   reads=[cw], part=False)
                            t3b = tp.next()
                            fw.stt(t3b, t3b[:, :], u, u[:, 0:512], cw[:, ch, 0:1], t2, t2[:, :], ALU.mult, ALU.add,
                                   reads=[cw], part=False)
                            t3.append(t3b)
                        sa = sap.next()
                        fw.act(sa, sa[:, :], t3[0], t3[0][:, :], AF.Silu, part=False)
                        fw.tt(mT, mT[:, j, tt * 512:(tt + 1) * 512], sa, sa[:, :], t3[1], t3[1][:, :], ALU.mult,
                              eng="pool", part=True)
                for tt in range(ST // 512):
                    t0 = st * ST + tt * 512
                    for dc in range(8):
                        x1 = x1p.next()
                        fw.dma(x1, x1[:, :], Xin, Xin[dc * 128:(dc + 1) * 128, t0:t0 + 512])
                        ps = self.pp.next()
                        for m in range(22):
                            fw.mm(lambda e, m=m, ps=ps, dc=dc, tt=tt: e.matmul(
                                ps[:, :], wd[:, m, dc * 128:(dc + 1) * 128], mT[:, m, tt * 512:(tt + 1) * 512],
                                start=(m == 0), stop=(m == 21)), reads=[wd, mT], writes=[ps], last=(m == 21))
                        x2 = x2p.next()
                        gb, gap = self.G2(li, dc)
                        fw.stt(x2, x2[:, :], ps, ps[:, :], gap, x1, x1[:, :], ALU.mult, ALU.add, reads=[gb],
                               part=False)
                        fw.dma(Xout, Xout[dc * 128:(dc + 1) * 128, t0:t0 + 512], x2, x2[:, :], part=True)

    def phase_A1(self):
        fw, I, S, NT = self.fw, self.I, self.S, self.NT
        Xin = S["X2"]
        with fw.phase():
            self.mk_norm_pools()
            hT = fw.sb([128, 8, 2048], BF16, "hT")
            xp = fw.pool(2, [128, 8, 512], F32, "xt")
            wfp = fw.pool(2, [128, 8, 512], F32, "wf")
            wbp = fw.pool(2, [128, 8, 512], BF16, "wb")
            stgp = fw.pool(3, [128, 2048], F32, "stgf")
            vstp = fw.pool(4, [128, 512], BF16, "vst")
            wgf = fw.sb([128, 8, 16], F32, "wgf")
            fw.dma(wgf, wgf[:, :, :], I["w_glr"], I["w_glr"][:, :, :])
            wgl = fw.sb([128, 8, 16], BF16, "wgl")
            fw.copy("pool", wgl, wgl[:, :, :], wgf, wgf[:, :, :], part=False)
            g16p = fw.pool(2, [16, 2048], BF16, "g16")
            full3 = lambda b: b[:, :, :]
            for st in range(NT // 2048):
                tcs = slice(st * 2048, (st + 1) * 2048)
                for tt in range(4):
                    xt = xp.next()
                    fw.dma(xt, xt[:, :, :], Xin, self.kview(Xin, st * 2048 + tt * 512, 512))
                    self.norm_mod(xt, lambda k: self.A1(1, k), lambda k: self.B1(1, k), hT,
                                  lambda k, tt=tt: hT[:, k, tt * 512:(tt + 1) * 512])
                g16 = g16p.next()
                for tt in range(4):
                    ps = self.pp.next()
                    for k in range(8):
                        fw.mm(lambda e, k=k, ps=ps, tt=tt: e.matmul(
                            ps[0:16, :], wgl[:, k, :], hT[:, k, tt * 512:(tt + 1) * 512],
                            start=(k == 0), stop=(k == 7)), reads=[wgl, hT], writes=[ps], last=(k == 7))
                    fw.copy("act", g16, g16[:, tt * 512:(tt + 1) * 512], ps, ps[0:16, :], part=(tt > 0))
                fw.dma(S["G1T"], S["G1T"][:, tcs], g16, g16[:, :], part=True)
                nxt = self.load_w_bf(wfp, wbp, I["w_in_b"], I["w_in_b"][0], full3)
                for blk in range(6):
                    wb = nxt
                    if blk + 1 < 6:
                        nxt = self.load_w_bf(wfp, wbp, I["w_in_b"], I["w_in_b"][blk + 1], full3)
                    if blk in (0, 1, 4, 5):
                        for fc in range(4):
                            stg = stgp.next()
                            for tt in range(4):
                                def evac(ps, stg=stg, tt=tt, blk=blk):
                                    oap = stg[:, tt * 512:(tt + 1) * 512]
                                    if blk == 0:
                                        fw.act(stg, oap, ps, ps[:, :], AF.Copy, scale=128.0 ** -0.5, part=(tt > 0))
                                    elif blk == 1:
                                        fw.copy("dve", stg, oap, ps, ps[:, :], part=(tt > 0))
                                    else:
                                        fw.act(stg, oap, ps, ps[:, :], AF.Silu, part=(tt > 0))
                                self.proj_fm(wb, fc, hT, tt, evac)
                            if blk == 0:
                                dst, r0 = S["Q1T"], fc * 128
                            elif blk == 1:
                                dst, r0 = S["K1T"], fc * 128
                            else:
                                dst, r0 = S["R1T"], (blk - 4) * 512 + fc * 128
                            fw.dma(dst, dst[r0:r0 + 128, tcs], stg, stg[:, :], part=True)
                    else:
                        half = blk - 2
                        dst = S["VB"]
                        for tb in range(16):
                            ps = self.pp.next()
                            for k in range(8):
                                fw.mm(lambda e, k=k, ps=ps, tb=tb, wb=wb: e.matmul(
                                    ps[:, :], hT[:, k, tb * 128:(tb + 1) * 128], wb[:, k, :],
                                    start=(k == 0), stop=(k == 7)),
                                    reads=[wb, hT], writes=[ps], last=(k == 7))
                            vst = vstp.next()
                            fw.copy("act" if tb % 2 else "dve", vst, vst[:, :], ps, ps[:, :], part=False)
                            tok0 = st * 2048 + tb * 128
                            fw.dma(dst, dst[tok0:tok0 + 128, half * 512:(half + 1) * 512], vst, vst[:, :], part=True)

    def phase_B1(self):
        fw, I, S, NT = self.fw, self.I, self.S, self.NT
        pp = self.pp
        with fw.phase():
            wgf = fw.sb([16, 512], F32, "wgatef")
            fw.dma(wgf, wgf[:, :], I["w_gate"], I["w_gate"][:, :])
            wg = fw.sb([16, 512], BF16, "wgate")
            fw.copy("pool", wg, wg[:, :], wgf, wgf[:, :], part=False)
            bg = fw.sb([128, 4], F32, "bg")
            fw.dma(bg, bg[:, :], I["b_gate"], I["b_gate"][:, :])
            nbg = fw.sb([128, 4], F32, "nbg")
            fw.op("dve", lambda e: e.tensor_scalar(out=nbg[:, :], in0=bg[:, :], scalar1=-1.0, scalar2=None,
                                                   op0=ALU.mult), reads=[bg], writes=[nbg])
            gn = fw.sb([128, 2], F32, "gn")
            fw.dma(gn, gn[:, :], I["gnorm"], I["gnorm"][:, :])
            maskc = fw.sb([128, 512], F32, "maskc")
            fw.dma(maskc, maskc[:, :], I["maskc"], I["maskc"][:, :])
            bmask = fw.sb([128, 128], F32, "bmask")
            fw.dma(bmask, bmask[:, :], I["bmask"], I["bmask"][:, :])
            Sf = [fw.pool(2, [128, 256], F32, f"Sf{h}") for h in range(4)]
            Sb = [fw.pool(12, [128, 256], BF16, f"Sb{h}") for h in range(4)]
            Sf_cur, Sb_cur = [], []
            for h in range(4):
                a, b = Sf[h].next(), Sb[h].next()
                fw.memset("pool", a, a[:, :], 0.0, part=False)
                fw.memset("pool", b, b[:, :], 0.0, part=False)
                Sf_cur.append(a)
                Sb_cur.append(b)
            glp = fw.pool(2, [16, 512], BF16, "glr")
            qfp = fw.pool(2, [128, 4, 512], F32, "qf")
            kfp = fw.pool(2, [128, 4, 512], F32, "kf")
            vtp = fw.pool(2, [128, 4, 1024], BF16, "vt")
            rtp = fw.pool(2, [128, 2, 512], F32, "rt")
            ogp = fw.pool(2, [128, 8, 512], BF16, "og")
            ofp = fw.pool(1, [128, 8, 512], F32, "of")
            tE = {n: fw.pool(2, [128, 512], F32, n) for n in ("te", "tsp", "tcum", "teb", "tenb", "tdiff", "ted")}
            tB = {n: fw.pool(2, [128, 512], BF16, n) for n in ("qt", "kt", "kd", "AT", "kdT")}
            sqp = fw.pool(2, [128, 2, 512], BF16, "sq2")
            stdp = fw.pool(2, [128, 512], F32, "std")
            rstdp = fw.pool(2, [128, 512], F32, "rstd")
            tmpp = fw.pool(2, [128, 512], F32, "tmp")
            G1T, Q1T, K1T, V1, R1T, OT = S["G1T"], S["Q1T"], S["K1T"], S["VB"], S["R1T"], S["OT"]
            Q1v = Q1T.t.ap().rearrange("(h p) t -> p h t", p=128)
            K1v = K1T.t.ap().rearrange("(h p) t -> p h t", p=128)
            V1v = V1.t.ap().rearrange("(b p) f -> p b f", p=128)
            R1v = R1T.t.ap().rearrange("(c p) t -> p c t", p=128)

            def load(tt):
                cs = slice(tt * 512, (tt + 1) * 512)
                gl, qf, kf, vt = glp.next(), qfp.next(), kfp.next(), vtp.next()
                fw.dma(gl, gl[:, :], G1T, G1T[:, cs])
                fw.dma(qf, qf[:, :, :], Q1T, Q1v[:, :, cs])
                fw.dma(kf, kf[:, :, :], K1T, K1v[:, :, cs])
                fw.dma(vt, vt[:, :, :], V1, V1v[:, tt * 4:(tt + 1) * 4, :])
                return gl, qf, kf, vt
            nxt = load(0)
            for tt in range(NT // 512):
                cs = slice(tt * 512, (tt + 1) * 512)
                gl, qf, kf, vt = nxt
                if tt + 1 < NT // 512:
                    nxt = load(tt + 1)
                og = ogp.next()
                of = ofp.next()
                for h in range(4):
                    psz = pp.next()
                    fw.mm(lambda e, psz=psz, h=h, gl=gl: e.matmul(psz[:, :], wg[:, h * 128:(h + 1) * 128], gl[:, :],
                                                                start=True, stop=True),
                          reads=[wg, gl], writes=[psz])
                    te = tE["te"].next()
                    fw.act(te, te[:, :], psz, psz[:, :], AF.Exp, scale=-1.0, bias=nbg[:, h:h + 1], reads=[nbg],
                           part=False)
                    tsp = tE["tsp"].next()
                    fw.act(tsp, tsp[:, :], te, te[:, :], AF.Ln, bias=1.0, part=False)
                    tcum = tE["tcum"].next()
                    fw.op("dve", lambda e, tcum=tcum, tsp=tsp: e.tensor_tensor_scan(
                        out=tcum[:, :], data0=maskc[:, :], data1=tsp[:, :], initial=0.0, op0=ALU.mult, op1=ALU.add),
                        reads=[maskc, tsp], writes=[tcum])
                    teb = tE["teb"].next()
                    fw.act(teb, teb[:, :], tcum, tcum[:, :], AF.Exp, scale=-1.0 / 16, part=False)
                    tenb = tE["tenb"].next()
                    fw.act(tenb, tenb[:, :], tcum, tcum[:, :], AF.Exp, scale=1.0 / 16, part=False)
                    tdiff = tE["tdiff"].next()
                    cv = tcum[:, :].rearrange("p (c t) -> p c t", t=64)
                    fw.tt(tdiff, tdiff[:, :].rearrange("p (c t) -> p c t", t=64), tcum,
                          cv[:, :, 63:64].to_broadcast([128, 8, 64]), tcum, cv, ALU.subtract, part=False)
                    ted = tE["ted"].next()
                    fw.act(ted, ted[:, :], tdiff, tdiff[:, :], AF.Exp, scale=-1.0 / 16, part=False)
                    qt, kt, kd = tB["qt"].next(), tB["kt"].next(), tB["kd"].next()
                    fw.tt(qt, qt[:, :], qf, qf[:, h, :], teb, teb[:, :], ALU.mult, part=False)
                    fw.tt(kt, kt[:, :], kf, kf[:, h, :], tenb, tenb[:, :], ALU.mult, part=False)
                    fw.tt(kd, kd[:, :], kf, kf[:, h, :], ted, ted[:, :], ALU.mult, part=False)
                    pst = self.pst
                    for b in range(4):
                        fw.mm(lambda e, b=b, kd=kd: e.transpose(pst[:, b * 128:(b + 1) * 128],
                                                                kd[:, b * 128:(b + 1) * 128], self.ident[:, :]),
                              reads=[kd, self.ident], writes=[pst], last=(b == 3))
                    kdT = tB["kdT"].next()
                    fw.copy("act", kdT, kdT[:, :], pst, pst[:, 0:512], part=False)
                    psA = pp.next()
                    for b in range(4):
                        bs = slice(b * 128, (b + 1) * 128)
                        fw.mm(lambda e, bs=bs, kt=kt, qt=qt, psA=psA: e.matmul(psA[:, bs], kt[:, bs], qt[:, bs],
                                                                            start=True, stop=True),
                              reads=[kt, qt], writes=[psA], last=(b == 3))
                    AT = tB["AT"].next()
                    fw.tt(AT, AT[:, :].rearrange("p (b t) -> p b t", t=128), psA,
                          psA[:, :].rearrange("p (b t) -> p b t", t=128), bmask,
                          bmask[:, :].unsqueeze(1).to_broadcast([128, 4, 128]), ALU.mult, part=False)
                    Sb_n = [Sb_cur[h]]
                    for n in range(8):
                        b, jj = divmod(n, 2)
                        rows = slice(jj * 64, jj * 64 + 64)
                        pskv = pp.next()
                        fw.mm(lambda e, pskv=pskv, rows=rows, b=b, kdT=kdT, vt=vt, h=h: e.matmul(
                            pskv[:, 0:256], kdT[rows, b * 128:(b + 1) * 128], vt[rows, b, h * 256:(h + 1) * 256],
                            start=True, stop=True), reads=[kdT, vt], writes=[pskv])
                        sf_new = Sf[h].next()
                        fw.stt(sf_new, sf_new[:, :], Sf_cur[h], Sf_cur[h][:, :], teb[:, n * 64 + 63:n * 64 + 64],
                               pskv, pskv[:, 0:256], ALU.mult, ALU.add, reads=[teb], part=False)
                        sb_new = Sb[h].next()
                        fw.copy("act", sb_new, sb_new[:, :], sf_new, sf_new[:, :], part=False)
                        Sf_cur[h] = sf_new
                        Sb_n.append(sb_new)
                    Sb_cur[h] = Sb_n[8]
                    for ec in range(2):
                        pso = pp.next()
                        for b in range(4):
                            fw.mm(lambda e, pso=pso, b=b, vt=vt, AT=AT, h=h, ec=ec: e.matmul(
                                pso[:, b * 128:(b + 1) * 128], vt[:, b, h * 256 + ec * 128:h * 256 + (ec + 1) * 128],
                                AT[:, b * 128:(b + 1) * 128], start=True, stop=False),
                                reads=[vt, AT], writes=[pso], last=False)
                            for jj in range(2):
                                n = 2 * b + jj
                                c0 = b * 128 + jj * 64
                                sbn = Sb_n[n]
                                fw.mm(lambda e, pso=pso, c0=c0, sbn=sbn, qt=qt, ec=ec, jj=jj: e.matmul(
                                    pso[:, c0:c0 + 64], sbn[:, ec * 128:(ec + 1) * 128], qt[:, c0:c0 + 64],
                                    start=False, stop=(jj == 1)),
                                    reads=[sbn, qt], writes=[pso], last=(jj == 1))
                        fw.copy("act", of, of[:, 2 * h + ec, :], pso, pso[:, :], part=True)
                    rt = rtp.next()
                    fw.dma(rt, rt[:, :, :], R1T, R1v[:, 2 * h:2 * h + 2, cs])
                    sq = sqp.next()
                    fw.act(sq, sq[:, :, :], of, of[:, 2 * h:2 * h + 2, :], AF.Square, part=False)
                    pss = pp.next()
                    for ec in range(2):
                        fw.mm(lambda e, ec=ec, sq=sq, pss=pss: e.matmul(pss[:, :], self.ones_bf[:, :], sq[:, ec, :],
                                                                        start=(ec == 0), stop=(ec == 1)),
                              reads=[sq, self.ones_bf], writes=[pss], last=(ec == 1))
                    std = stdp.next()
                    fw.act(std, std[:, :], pss, pss[:, :], AF.Ln, scale=1.0 / 256, bias=EPS, part=False)
                    rstd = rstdp.next()
                    fw.act(rstd, rstd[:, :], std, std[:, :], AF.Exp, scale=-0.5, part=False)
                    for ec in range(2):
                        tmp = tmpp.next()
                        fw.stt(tmp, tmp[:, :], of, of[:, 2 * h + ec, :], gn[:, ec:ec + 1], rstd, rstd[:, :],
                               ALU.mult, ALU.mult, reads=[gn], part=False)
                        fw.tt(og, og[:, 2 * h + ec, :], tmp, tmp[:, :], rt, rt[:, ec, :], ALU.mult,
                              part=not (h == 0 and ec == 0))
                fw.dma(OT, self.kview(OT, tt * 512, 512), og, og[:, :, :], part=True)

    def phase_F(self, Xin):
        fw, NT = self.fw, self.NT
        with fw.phase():
            xp = fw.pool(2, [128, 8, 512], F32, "xt")
            sqp = fw.pool(2, [128, 8, 512], BF16, "sq")
            stdp = fw.pool(2, [128, 512], F32, "std")
            rstdp = fw.pool(2, [128, 512], F32, "rstd")
            yp = fw.pool(2, [128, 8, 512], F32, "y")

            def load(tt):
                xt = xp.next()
                fw.dma(xt, xt[:, :, :], Xin, self.kview(Xin, tt * 512, 512))
                return xt
            nxt = load(0)
            for tt in range(NT // 512):
                xt = nxt
                if tt + 1 < NT // 512:
                    nxt = load(tt + 1)
                sq = sqp.next()
                fw.act(sq, sq[:, :, :], xt, xt[:, :, :], AF.Square, part=False)
                pss = self.pp.next()
                for k in range(8):
                    fw.mm(lambda e, k=k, sq=sq, pss=pss: e.matmul(pss[:, :], self.ones_bf[:, :], sq[:, k, :],
                                                                   start=(k == 0), stop=(k == 7)),
                          reads=[sq, self.ones_bf], writes=[pss], last=(k == 7))
                std = stdp.next()
                fw.act(std, std[:, :], pss, pss[:, :], AF.Ln, scale=1.0 / 1024, bias=EPS, part=False)
                rstd = rstdp.next()
                fw.act(rstd, rstd[:, :], std, std[:, :], AF.Exp, scale=-0.5, part=False)
                y = yp.next()
                for k in range(8):
                    fw.stt(y, y[:, k, :], xt, xt[:, k, :], self.nfin[:, k:k + 1], rstd, rstd[:, :],
                           ALU.mult, ALU.mult, reads=[self.nfin], part=(k > 0))
                fw.dma(self.OUT, self.kview(self.OUT, tt * 512, 512), y, y[:, :, :], part=True)


def _t5_bucket(dist):
    n_buckets, max_distance = 32, 2048
    max_exact = n_buckets // 2
    n = np.maximum(dist, max_exact).astype(np.float32)
    large = max_exact + (np.log(n / np.float32(max_exact)) / np.float32(math.log(max_distance / max_exact))
                         * np.float32(n_buckets - max_exact)).astype(np.int32)
    large = np.minimum(large, n_buckets - 1)
    return np.where(dist < max_exact, dist, large)


def _kp(w):
    return np.ascontiguousarray(w.reshape(8, 128, -1).transpose(1, 0, 2))


def prep_shared(inp):
    f = lambda a: np.ascontiguousarray(np.asarray(a, dtype=np.float32))
    o = {}
    w_ada = f(inp["w_ada"])
    o["w_ada"] = np.ascontiguousarray(w_ada.reshape(2, 8, 128, 6, 1024).transpose(0, 3, 2, 1, 4))
    o["b_ada"] = np.ascontiguousarray(f(inp["b_ada"]).reshape(2, 48, 128).transpose(0, 2, 1))
    o["nmix"] = np.ascontiguousarray(f(inp["norm_mix"]).reshape(2, 8, 128).transpose(0, 2, 1))
    o["nffn"] = np.ascontiguousarray(f(inp["norm_ffn"]).reshape(2, 8, 128).transpose(0, 2, 1))
    o["nfin"] = np.ascontiguousarray(f(inp["norm_final"]).reshape(8, 128).T)
    wia = f(inp["w_in_a"])[0]
    o["w_in_a"] = np.ascontiguousarray(wia.reshape(8, 128, 18, 512).transpose(2, 1, 0, 3))
    o["w_out_a"] = _kp(f(inp["w_out_a"])[0])
    rb = f(inp["rel_bias"])
    k = np.arange(128)[:, None]
    q = np.arange(128)[None, :]
    tb = np.empty((3, 8, 128, 2, 2, 128), np.float32)
    for g, (win, d) in enumerate(A_CONFIGS):
        for pc in range(2):
            steps = (q - k) + (128 if pc == 0 else 0)
            valid = (steps >= 0) & (steps <= 128)
            bucket = _t5_bucket(np.clip(steps, 0, 128) * d)
            for c in range(8):
                for hh in range(2):
                    h = 2 * c + hh
                    tb[g, c, :, hh, pc, :] = np.where(valid, rb[bucket, g * 16 + h], np.float32(-30000.0))
    o["tbias"] = tb.reshape(3, 8, 128, 512)
    wib = f(inp["w_in_b"])[0]
    blocks = [wib[:, 0:512], wib[:, 512:1024], wib[:, 1024:1536], wib[:, 1536:2048],
              wib[:, 2064:2576], wib[:, 2576:3088]]
    o["w_in_b"] = np.stack([_kp(b) for b in blocks])
    o["w_glr"] = _kp(wib[:, 2048:2064])
    o["w_gate"] = f(inp["w_gate_b"])[0]
    o["b_gate"] = np.ascontiguousarray(f(inp["b_gate_b"])[0].reshape(4, 128).T)
    o["gnorm"] = np.ascontiguousarray(f(inp["gnorm_b"])[0].reshape(2, 128).T)
    o["w_out_b"] = _kp(f(inp["w_out_b"])[0])
    wup = f(inp["w_up"])
    o["w_up"] = np.ascontiguousarray(wup.reshape(2, 8, 128, 2, 22, 128).transpose(0, 4, 2, 3, 1, 5))
    cwt = f(inp["conv_w"])
    cb = f(inp["conv_b"])
    cw = np.concatenate([cwt, cb[:, None, :]], axis=1)
    o["cw"] = np.ascontiguousarray(cw.reshape(2, 4, 44, 128).transpose(0, 3, 2, 1))
    o["w_down"] = np.ascontiguousarray(f(inp["w_down"]).reshape(2, 22, 128, 1024))
    mc = np.ones((128, 512), np.float32)
    mc[:, 0::64] = 0.0
    o["maskc"] = mc
    s = np.arange(128)[:, None]
    t = np.arange(128)[None, :]
    o["bmask"] = ((s // 64 == t // 64) & (s <= t)).astype(np.float32)
    o["identf"] = np.eye(128, dtype=np.float32)
    return o


def prep_core(inp, b, NT):
    x = np.asarray(inp["x"], dtype=np.float32)[b, :NT]
    c = np.asarray(inp["c"], dtype=np.float32)[b]
    return {"xT": np.ascontiguousarray(x.T), "cT": np.ascontiguousarray(c.reshape(8, 128).T)}


_PROG_CACHE = {}


def get_prog(NT, dbg=(), upto=99):
    key = (NT, tuple(dbg), upto)
    if key not in _PROG_CACHE:
        _PROG_CACHE[key] = Prog(NT, dbg, upto)
    return _PROG_CACHE[key]


def kernel(**inputs):
    NT = 8192
    prog = get_prog(NT)
    shared = prep_shared(inputs)
    in_maps = []
    real = {0: 0, 1: 1, 4: 2, 5: 3}
    zeros = None
    for core in range(8):
        if core in real:
            m = dict(shared)
            m.update(prep_core(inputs, real[core], NT))
        else:
            if zeros is None:
                zeros = {k: np.zeros_like(v) for k, v in shared.items()}
                zeros.update({k: np.zeros_like(v) for k, v in prep_core(inputs, 0, NT).items()})
            m = zeros
        in_maps.append(m)
    res = run_bass_kernel_spmd(prog.nc, in_maps, core_ids=list(range(8)))
    out = np.empty((4, NT, 1024), np.float32)
    for core, b in real.items():
        out[b] = res.results[core]["yT"].T
    return out
```

```python
import math
import numpy as np
from contextlib import ExitStack, contextmanager
import concourse.bass as bass
import concourse.mybir as mybir
from concourse.bass_utils import run_bass_kernel_spmd

F32 = mybir.dt.float32
BF16 = mybir.dt.bfloat16
AF = mybir.ActivationFunctionType
ALU = mybir.AluOpType

ENGS = ("pe", "act", "dve", "pool", "sp")
EPS = 1e-6
A_CONFIGS = ((128, 1), (512, 4), (2048, 16))
D_FF = 2816


class Buf:
    def __init__(self, t, name):
        self.t = t
        self.name = name
        self.w = {}
        self.r = {}
        self.dsem = None

    def __getitem__(self, idx):
        return self.t[idx]


class Pool:
    def __init__(self, bufs):
        self.bufs = bufs
        self.i = 0

    def next(self):
        b = self.bufs[self.i % len(self.bufs)]
        self.i += 1
        return b


class FW:
    def __init__(self, nc, es, dbg=()):
        self.nc = nc
        self.es = es
        self.cur = es
        self.ops = {e: [] for e in ENGS}
        self.cnt = {e: 0 for e in ENGS}
        self.seen = {e: {} for e in ENGS}
        self.sems = {}
        self.dcount = {}
        self.free_dsems = []
        self.phase_bufs = None
        self.dbg = set(dbg)
        self.dbg_out = []
        for e in ENGS:
            self.sems[("eng", e)] = es.enter_context(nc.semaphore("sem_" + e))
        self.nbuf = 0
        self.ndsem = 0

    def sb(self, shape, dt, name=None):
        self.nbuf += 1
        name = (name or "sb") + f"_{self.nbuf}"
        t = self.cur.enter_context(self.nc.sbuf_tensor(name, list(shape), dt))
        b = Buf(t, name)
        if self.phase_bufs is not None:
            self.phase_bufs.append(b)
        return b

    def ps(self, shape, dt=F32, name=None):
        self.nbuf += 1
        name = (name or "ps") + f"_{self.nbuf}"
        t = self.es.enter_context(self.nc.psum_tensor(name, list(shape), dt))
        return Buf(t, name)

    def pool(self, n, shape, dt, name=None):
        return Pool([self.sb(shape, dt, name) for _ in range(n)])

    def dram(self, name, shape, dt):
        kind = "Internal"
        if name in self.dbg:
            kind = "ExternalOutput"
            self.dbg_out.append(name)
        t = self.nc.dram_tensor(name, list(shape), dt, kind=kind)
        return Buf(t, name)

    def ext(self, name, shape, dt, kind):
        t = self.nc.dram_tensor(name, list(shape), dt, kind=kind)
        return Buf(t, name)

    def _dsem(self, b):
        if b.dsem is None:
            if self.free_dsems:
                b.dsem = self.free_dsems.pop()
            else:
                self.ndsem += 1
                key = ("dma", self.ndsem)
                self.sems[key] = self.es.enter_context(self.nc.semaphore(f"dsem{self.ndsem}"))
                self.dcount[key] = 0
                b.dsem = key
        return b.dsem

    @contextmanager
    def phase(self):
        with ExitStack() as pes:
            self.cur = pes
            self.phase_bufs = []
            yield
            self.barrier()
            for b in self.phase_bufs:
                if b.dsem is not None:
                    self.free_dsems.append(b.dsem)
            self.phase_bufs = None
            self.cur = self.es

    def barrier(self):
        for e in ENGS:
            need = {}
            for e2 in ENGS:
                if e2 != e and self.cnt[e2] > 0:
                    need[("eng", e2)] = self.cnt[e2]
            for k, v in self.dcount.items():
                if v > 0:
                    need[k] = v
            waits = []
            seen = self.seen[e]
            for k, v in need.items():
                if seen.get(k, 0) >= v:
                    continue
                seen[k] = v
                waits.append((k, v))
            self.ops[e].append((waits, None, None, 0))

    def _waits(self, eng, reads, writes, part):
        need = {}

        def add(evs):
            for k, v in evs.items():
                if need.get(k, 0) < v:
                    need[k] = v
        for b in reads:
            add(b.w)
        for b in writes:
            add(b.r)
            add(b.w)
        out = []
        seen = self.seen[eng]
        for k, v in need.items():
            if eng == "pe" and k == ("eng", "pe"):
                continue
            if seen.get(k, 0) >= v:
                continue
            seen[k] = v
            out.append((k, v))
        return out

    @staticmethod
    def _commit(ev, reads, writes, part):
        k, v = ev
        for b in reads:
            if b.r.get(k, 0) < v:
                b.r[k] = v
        for b in writes:
            if part:
                if b.w.get(k, 0) < v:
                    b.w[k] = v
            else:
                b.w = {k: v}
                b.r = {}

    def op(self, eng, fn, reads=(), writes=(), part=False):
        waits = self._waits(eng, reads, writes, part)
        self.cnt[eng] += 1
        ev = (("eng", eng), self.cnt[eng])
        self.ops[eng].append((waits, fn, ev[0], 1))
        self._commit(ev, reads, writes, part)

    def mm(self, fn, reads=(), writes=(), last=True):
        waits = self._waits("pe", reads, writes, True)
        if last:
            self.cnt["pe"] += 1
            ev = (("eng", "pe"), self.cnt["pe"])
            self.ops["pe"].append((waits, fn, ev[0], 1))
        else:
            ev = (("eng", "pe"), self.cnt["pe"] + 1)
            self.ops["pe"].append((waits, fn, None, 0))
        self._commit(ev, reads, writes, True)

    def dma(self, out_b, out_ap, in_b, in_ap, part=False, q="sp"):
        primary = out_b if isinstance(out_b.t, bass.SBTensorHandle) else in_b
        key = self._dsem(primary)
        waits = self._waits(q, [in_b], [out_b], part)
        self.dcount[key] += 16
        ev = (key, self.dcount[key])

        def fn(e, out_ap=out_ap, in_ap=in_ap):
            return e.dma_start(out=out_ap, in_=in_ap)
        self.ops[q].append((waits, fn, key, 16))
        self._commit(ev, [in_b], [out_b], part)

    def final_wait(self, eng, bufs):
        need = {}
        for b in bufs:
            for d in (b.w, b.r):
                for k, v in d.items():
                    if need.get(k, 0) < v:
                        need[k] = v
        self.ops[eng].append((list(need.items()), None, None, 0))

    def act(self, out_b, out_ap, in_b, in_ap, func, scale=None, bias=None, reads=(), part=True):
        kw = {}
        if scale is not None:
            kw["scale"] = scale
        if bias is not None:
            kw["bias"] = bias
        self.op("act", lambda e: e.activation(out=out_ap, in_=in_ap, func=func, **kw),
                reads=[in_b, *reads], writes=[out_b], part=part)

    def tt(self, out_b, out_ap, a_b, a_ap, b_b, b_ap, op, eng="dve", part=True):
        self.op(eng, lambda e: e.tensor_tensor(out=out_ap, in0=a_ap, in1=b_ap, op=op),
                reads=[a_b, b_b], writes=[out_b], part=part)

    def stt(self, out_b, out_ap, a_b, a_ap, scalar, b_b, b_ap, op0, op1, reads=(), part=True):
        self.op("dve", lambda e: e.scalar_tensor_tensor(out=out_ap, in0=a_ap, scalar=scalar, in1=b_ap,
                                                         op0=op0, op1=op1),
                reads=[a_b, b_b, *reads], writes=[out_b], part=part)

    def copy(self, eng, out_b, out_ap, in_b, in_ap, part=True):
        if eng == "act":
            self.op("act", lambda e: e.activation(out=out_ap, in_=in_ap, func=AF.Copy),
                    reads=[in_b], writes=[out_b], part=part)
        else:
            self.op(eng, lambda e: e.tensor_copy(out=out_ap, in_=in_ap), reads=[in_b], writes=[out_b], part=part)

    def memset(self, eng, b, ap, val, part=True):
        self.op(eng, lambda e: e.memset(ap, val), writes=[b], part=part)

    def emit(self):
        nc = self.nc
        sems = self.sems
        ops = self.ops
        with nc.Block() as block:
            def run(e, lst):
                for waits, fn, inc, n in lst:
                    for k, v in waits:
                        e.wait_ge(sems[k], v)
                    if fn is None:
                        continue
                    inst = fn(e)
                    if inc is not None:
                        inst.then_inc(sems[inc], n)

            @block.tensor
            def _(e):
                run(e, ops["pe"])

            @block.scalar
            def _(e):
                run(e, ops["act"])

            @block.vector
            def _(e):
                run(e, ops["dve"])

            @block.gpsimd
            def _(e):
                run(e, ops["pool"])

            @block.sync
            def _(e):
                run(e, ops["sp"])


class Prog:
    def __init__(self, NT, dbg=(), upto=99):
        self.NT = NT
        self.upto = upto
        nc = bass.Bass("TRN2", target_bir_lowering=False)
        self.nc = nc
        with ExitStack() as es:
            fw = FW(nc, es, dbg)
            self.fw = fw
            self.declare_io()
            self.consts()
            self.run()
            fw.emit()

    def kview(self, D, t0, n):
        return D.t.ap().rearrange("(k p) t -> p k t", p=128)[:, :, t0:t0 + n]

    def declare_io(self):
        fw, NT = self.fw, self.NT
        I = {}

        def inp(name, shape):
            I[name] = fw.ext(name, shape, F32, "ExternalInput")
        inp("xT", [1024, NT])
        inp("cT", [128, 8])
        inp("w_ada", [2, 6, 128, 8, 1024])
        inp("b_ada", [2, 128, 48])
        inp("nmix", [2, 128, 8])
        inp("nffn", [2, 128, 8])
        inp("nfin", [128, 8])
        inp("w_in_a", [18, 128, 8, 512])
        inp("w_out_a", [128, 8, 1024])
        inp("tbias", [3, 8, 128, 512])
        inp("w_in_b", [6, 128, 8, 512])
        inp("w_glr", [128, 8, 16])
        inp("w_gate", [16, 512])
        inp("b_gate", [128, 4])
        inp("gnorm", [128, 2])
        inp("w_out_b", [128, 8, 1024])
        inp("w_up", [2, 22, 128, 2, 8, 128])
        inp("cw", [2, 128, 44, 4])
        inp("w_down", [2, 22, 128, 1024])
        inp("maskc", [128, 512])
        inp("bmask", [128, 128])
        inp("identf", [128, 128])
        self.I = I
        self.OUT = fw.ext("yT", [1024, NT], F32, "ExternalOutput")
        S = {}
        for g in range(3):
            S[f"QT{g}"] = fw.dram(f"QT{g}", [1024, NT], BF16)
            S[f"KT{g}"] = fw.dram(f"KT{g}", [1024, NT], BF16)
            S[f"V{g}"] = fw.dram(f"V{g}", [NT, 16, 128], BF16)
        S["OT"] = fw.dram("OT", [1024, NT], BF16)
        S["H2"] = fw.dram("H2", [1024, NT], BF16)
        for i in range(1, 5):
            S[f"X{i}"] = fw.dram(f"X{i}", [1024, NT], F32)
        S["Q1T"] = fw.dram("Q1T", [512, NT], F32)
        S["K1T"] = fw.dram("K1T", [512, NT], F32)
        S["VB"] = fw.dram("VB", [NT, 1024], BF16)
        S["R1T"] = fw.dram("R1T", [1024, NT], F32)
        S["G1T"] = fw.dram("G1T", [16, NT], BF16)
        self.S = S

    def consts(self):
        fw, I = self.fw, self.I
        self.pp = Pool([fw.ps([128, 512], F32) for _ in range(7)])
        self.pst = fw.ps([128, 1024], BF16)
        self.ones_bf = fw.sb([128, 128], BF16, "ones")
        fw.memset("dve", self.ones_bf, self.ones_bf[:, :], 1.0, part=False)
        self.onesz = fw.sb([128, 2, 128], BF16, "onesz")
        fw.memset("dve", self.onesz, self.onesz[:, :, :], 0.0, part=False)
        fw.memset("dve", self.onesz, self.onesz[:, 0, 0:64], 1.0)
        fw.memset("dve", self.onesz, self.onesz[:, 1, 64:128], 1.0)
        idf = fw.sb([128, 128], F32, "identf")
        fw.dma(idf, idf[:, :], I["identf"], I["identf"][:, :])
        self.ident = fw.sb([128, 128], BF16, "ident")
        fw.copy("dve", self.ident, self.ident[:, :], idf, idf[:, :], part=False)
        self.modt = [fw.sb([128, 48], F32, f"modt{i}") for i in range(2)]
        self.Amod = [fw.sb([128, 16], F32, f"amod{i}") for i in range(2)]
        self.nfin = fw.sb([128, 8], F32, "nfin")
        fw.dma(self.nfin, self.nfin[:, :], I["nfin"], I["nfin"][:, :])

    def run(self):
        S, I = self.S, self.I
        up = self.upto
        self.phase_mod()
        if up >= 1:
            self.phase_A0()
        if up >= 2:
            self.phase_B0()
        if up >= 3:
            self.phase_C1(0, S["OT"], I["w_out_a"], I["xT"], S["X1"])
        if up >= 4:
            self.phase_C2(0, S["X1"], S["X2"])
        if up >= 5:
            self.phase_A1()
        if up >= 6:
            self.phase_B1()
        if up >= 7:
            self.phase_C1(1, S["OT"], I["w_out_b"], S["X2"], S["X3"])
        if up >= 8:
            self.phase_C2(1, S["X3"], S["X4"])
        if up >= 9:
            self.phase_F(S["X4"])
            self.fw.final_wait("sp", [self.OUT])
        else:
            self.fw.barrier()

    def phase_mod(self):
        fw, I, pp = self.fw, self.I, self.pp
        with fw.phase():
            ct = fw.sb([128, 8], F32)
            fw.dma(ct, ct[:, :], I["cT"], I["cT"][:, :])
            sc = fw.sb([128, 8], F32)
            fw.act(sc, sc[:, :], ct, ct[:, :], AF.Silu, part=False)
            wpool = fw.pool(2, [128, 8, 1024], F32, "wada")
            for i in range(2):
                psm = pp.next()
                for blk in range(6):
                    wt = wpool.next()
                    fw.dma(wt, wt[:, :, :], I["w_ada"], I["w_ada"][i, blk])
                    for fc in range(8):
                        col = blk * 8 + fc
                        for k in range(8):
                            fw.mm(lambda e, wt=wt, k=k, fc=fc, col=col, psm=psm: e.matmul(
                                psm[:, col:col + 1], wt[:, k, fc * 128:(fc + 1) * 128], sc[:, k:k + 1],
                                start=(k == 0), stop=(k == 7)),
                                reads=[wt, sc], writes=[psm], last=(k == 7))
                bt = fw.sb([128, 48], F32)
                fw.dma(bt, bt[:, :], I["b_ada"], I["b_ada"][i])
                nm = fw.sb([128, 16], F32)
                fw.dma(nm, nm[:, 0:8], I["nmix"], I["nmix"][i])
                fw.dma(nm, nm[:, 8:16], I["nffn"], I["nffn"][i], part=True)
                mt = self.modt[i]
                fw.tt(mt, mt[:, :], psm, psm[:, 0:48], bt, bt[:, :], ALU.add, part=False)
                am = self.Amod[i]
                fw.stt(am, am[:, 0:8], mt, mt[:, 8:16], 1.0, nm, nm[:, 0:8], ALU.add, ALU.mult, part=False)
                fw.stt(am, am[:, 8:16], mt, mt[:, 32:40], 1.0, nm, nm[:, 8:16], ALU.add, ALU.mult)

    def A1(self, i, k): return self.Amod[i], self.Amod[i][:, k:k + 1]
    def A2(self, i, k): return self.Amod[i], self.Amod[i][:, 8 + k:9 + k]
    def B1(self, i, k): return self.modt[i], self.modt[i][:, k:k + 1]
    def G1(self, i, k): return self.modt[i], self.modt[i][:, 16 + k:17 + k]
    def B2(self, i, k): return self.modt[i], self.modt[i][:, 24 + k:25 + k]
    def G2(self, i, k): return self.modt[i], self.modt[i][:, 40 + k:41 + k]

    def mk_norm_pools(self):
        fw = self.fw
        self.sqp = fw.pool(2, [128, 8, 512], BF16, "sq")
        self.stdp = fw.pool(2, [128, 512], F32, "std")
        self.rstdp = fw.pool(2, [128, 512], F32, "rstd")
        self.tmpp = fw.pool(3, [128, 512], F32, "ntmp")

    def norm_mod(self, xt, Af, Bf, out_b, out_ap):
        fw = self.fw
        sq = self.sqp.next()
        fw.act(sq, sq[:, :, :], xt, xt[:, :, :], AF.Square, part=False)
        pss = self.pp.next()
        for k in range(8):
            fw.mm(lambda e, k=k, sq=sq, pss=pss: e.matmul(pss[:, :], self.ones_bf[:, :], sq[:, k, :],
                                                           start=(k == 0), stop=(k == 7)),
                  reads=[sq, self.ones_bf], writes=[pss], last=(k == 7))
        std = self.stdp.next()
        fw.act(std, std[:, :], pss, pss[:, :], AF.Ln, scale=1.0 / 1024, bias=EPS, part=False)
        rstd = self.rstdp.next()
        fw.act(rstd, rstd[:, :], std, std[:, :], AF.Exp, scale=-0.5, part=False)
        for k in range(8):
            tmp = self.tmpp.next()
            ab, aap = Af(k)
            bb, bap = Bf(k)
            fw.stt(tmp, tmp[:, :], xt, xt[:, k, :], aap, rstd, rstd[:, :], ALU.mult, ALU.mult,
                   reads=[ab], part=False)
            fw.act(out_b, out_ap(k), tmp, tmp[:, :], AF.Identity, bias=bap, reads=[bb], part=True)

    def load_w_bf(self, fpool, bpool, src_b, src_ap, shape_ap):
        fw = self.fw
        wf = fpool.next()
        fw.dma(wf, shape_ap(wf), src_b, src_ap)
        wb = bpool.next()
        fw.copy("pool", wb, shape_ap(wb), wf, shape_ap(wf), part=False)
        return wb

    def proj_fm(self, wb, fc, hT, tt, evac):
        fw = self.fw
        ps = self.pp.next()
        for k in range(8):
            fw.mm(lambda e, k=k, ps=ps: e.matmul(ps[:, :], wb[:, k, fc * 128:(fc + 1) * 128],
                                                  hT[:, k, tt * 512:(tt + 1) * 512],
                                                  start=(k == 0), stop=(k == 7)),
                  reads=[wb, hT], writes=[ps], last=(k == 7))
        evac(ps)

    def phase_A0(self):
        fw, I, S, NT = self.fw, self.I, self.S, self.NT
        with fw.phase():
            self.mk_norm_pools()
            hT = fw.sb([128, 8, 2048], BF16, "hT")
            xp = fw.pool(2, [128, 8, 512], F32, "xt")
            wfp = fw.pool(2, [128, 8, 512], F32, "wf")
            wbp = fw.pool(2, [128, 8, 512], BF16, "wb")
            stgp = fw.pool(3, [128, 2048], BF16, "stg")
            vstp = fw.pool(4, [128, 8, 128], BF16, "vst")
            for b in vstp.bufs:
                fw.memset("pool", b, b[:, :, :], 1.0, part=False)
            full3 = lambda b: b[:, :, :]
            ev = [0]
            for st in range(NT // 2048):
                for tt in range(4):
                    xt = xp.next()
                    t0 = st * 2048 + tt * 512
                    fw.dma(xt, xt[:, :, :], I["xT"], self.kview(I["xT"], t0, 512))
                    self.norm_mod(xt, lambda k: self.A1(0, k), lambda k: self.B1(0, k), hT,
                                  lambda k, tt=tt: hT[:, k, tt * 512:(tt + 1) * 512])
                nxt = self.load_w_bf(wfp, wbp, I["w_in_a"], I["w_in_a"][0], full3)
                for blk in range(18):
                    wb = nxt
                    if blk + 1 < 18:
                        nxt = self.load_w_bf(wfp, wbp, I["w_in_a"], I["w_in_a"][blk + 1], full3)
                    g, j, half = blk // 6, (blk % 6) // 2, blk % 2
                    if j < 2:
                        dst = S[("QT" if j == 0 else "KT") + str(g)]
                        for fc in range(4):
                            stg = stgp.next()
                            for tt in range(4):
                                def evac(ps, stg=stg, tt=tt, j=j):
                                    ev[0] += 1
                                    oap = stg[:, tt * 512:(tt + 1) * 512]
                                    if j == 0:
                                        fw.act(stg, oap, ps, ps[:, :], AF.Copy, scale=0.125, part=(tt > 0))
                                    elif ev[0] % 2 == 0:
                                        fw.copy("dve", stg, oap, ps, ps[:, :], part=(tt > 0))
                                    else:
                                        fw.copy("act", stg, oap, ps, ps[:, :], part=(tt > 0))
                                self.proj_fm(wb, fc, hT, tt, evac)
                            r0 = half * 512 + fc * 128
                            fw.dma(dst, dst[r0:r0 + 128, st * 2048:(st + 1) * 2048], stg, stg[:, :], part=True)
                    else:
                        dst = S[f"V{g}"]
                        for tb in range(16):
                            ps = self.pp.next()
                            for k in range(8):
                                fw.mm(lambda e, k=k, ps=ps, tb=tb, wb=wb: e.matmul(
                                    ps[:, :], hT[:, k, tb * 128:(tb + 1) * 128], wb[:, k, :],
                                    start=(k == 0), stop=(k == 7)),
                                    reads=[wb, hT], writes=[ps], last=(k == 7))
                            vst = vstp.next()
                            psv = ps[:, :].rearrange("p (h e) -> p h e", e=64)
                            fw.copy("act" if tb % 2 == 0 else "dve", vst, vst[:, :, 0:64], ps, psv[:, :, :],
                                    part=True)
                            tok0 = st * 2048 + tb * 128
                            fw.dma(dst, dst[tok0:tok0 + 128, half * 8:(half + 1) * 8, :], vst, vst[:, :, :],
                                   part=True)

    def phase_B0(self):
        fw, I, S, NT = self.fw, self.I, self.S, self.NT
        NTL = NT // 2048
        LOOK = 2
        with fw.phase():
            accH = [fw.sb([128, NT], F32, "accH0"), fw.sb([128, NT], F32, "accH1")]
            tbp = fw.pool(2, [128, 512], F32, "tb")
            ebp = fw.pool(2, [128, 512], F32, "eB")
            kp = fw.pool(4, [128, 2048], BF16, "Kt")
            qp = fw.pool(3, [128, 2, 2048], BF16, "Qt")
            for b in qp.bufs:
                fw.memset("pool", b, b[:, :, :], 0.0, part=False)
            vp = fw.pool(4, [128, 16, 2, 128], BF16, "Vt")
            ptp = fw.pool(4, [128, 512], F32, "Pt")
            pbp = fw.pool(6, [128, 512], BF16, "Pb")
            obp = fw.pool(2, [128, 2048], BF16, "ob")
            lnp = fw.pool(2, [128, 2048], F32, "lnd")
            onesz = self.onesz
            import os
            _cs = [int(v) for v in os.environ.get("B0_C", "0,1,2,3,4,5,6,7").split(",") if v != "none"]
            _gs = [int(v) for v in os.environ.get("B0_G", "0,1,2").split(",")]
            nblk = [0]
            for c in _cs:
                for g in _gs:
                    d = A_CONFIGS[g][1]
                    nbq = 16 // d
                    tb = tbp.next()
                    fw.dma(tb, tb[:, :], I["tbias"], I["tbias"][g, c])
                    eB = ebp.next()
                    fw.act(eB, eB[:, :], tb, tb[:, :], AF.Exp, part=False)
                    KT, QT, V = S[f"KT{g}"], S[f"QT{g}"], S[f"V{g}"]
                    Vv = V.t.ap().rearrange("(t bq i r) h e -> t r i bq (h e)", bq=nbq, i=128, r=d)

                    def load_tile(t):
                        Kt, Qt, Vt = kp.next(), qp.next(), vp.next()
                        cs = slice(t * 2048, (t + 1) * 2048)
                        fw.dma(Kt, Kt[:, :], KT, KT[c * 128:(c + 1) * 128, cs])
                        fw.dma(Qt, Qt[0:64, 0, :], QT, QT[c * 128:c * 128 + 64, cs], part=True)
                        fw.dma(Qt, Qt[64:128, 1, :], QT, QT[c * 128 + 64:(c + 1) * 128, cs], part=True)
                        Vtv = Vt[:, :, :, :].rearrange("p j h e -> p j (h e)")
                        for r in range(d):
                            fw.dma(Vt, Vtv[:, r * nbq:(r + 1) * nbq, :], V,
                                   Vv[t, r, :, :, c * 256:(c + 1) * 256], part=(r > 0))
                        return Kt, Qt, Vt
                    tiles = {0: load_tile(0)}
                    blocks = [(t, j) for t in range(NTL) for j in range(16)]
                    st = {}
                    grp = {}

                    def qk_stage(i):
                        t, j = blocks[i]
                        if j == 0 and t + 1 < NTL:
                            tiles[t + 1] = load_tile(t + 1)
                        Kt, Qt, Vt = tiles[t]
                        r, bq = divmod(j, nbq)
                        s0 = r + bq * 128 * d
                        cols = slice(s0, s0 + 127 * d + 1, d)
                        if bq > 0:
                            pK, pV, pj, pcols = Kt, Vt, j - 1, slice(s0 - 128 * d, s0 - d + 1, d)
                        elif t > 0:
                            ps0 = r + (nbq - 1) * 128 * d
                            pK, pV = tiles[t - 1][0], tiles[t - 1][2]
                            pj, pcols = r * nbq + nbq - 1, slice(ps0, ps0 + 127 * d + 1, d)
                        else:
                            pK = pV = pj = pcols = None
                        psS = self.pp.next()
                        for hh in range(2):
                            if pK is not None:
                                fw.mm(lambda e, psS=psS, hh=hh, pK=pK, pcols=pcols, Qt=Qt, cols=cols:
                                      e.matmul(psS[:, (hh * 2) * 128:(hh * 2 + 1) * 128], pK[:, pcols],
                                               Qt[:, hh, cols], start=True, stop=True),
                                      reads=[pK, Qt], writes=[psS], last=False)
                            fw.mm(lambda e, psS=psS, hh=hh, Kt=Kt, Qt=Qt, cols=cols:
                                  e.matmul(psS[:, (hh * 2 + 1) * 128:(hh * 2 + 2) * 128], Kt[:, cols],
                                           Qt[:, hh, cols], start=True, stop=True),
                                  reads=[Kt, Qt], writes=[psS], last=(hh == 1))
                        Pt = ptp.next()
                        if pK is None:
                            Ptv = Pt[:, :].rearrange("p (h c q) -> p h c q", h=2, c=2)
                            psv = psS[:, :].rearrange("p (h c q) -> p h c q", h=2, c=2)
                            fw.memset("pool", Pt, Ptv[:, :, 0, :], 0.0, part=False)
                            fw.act(Pt, Ptv[:, :, 1, :], psS, psv[:, :, 1, :], AF.Exp, part=True)
                        else:
                            fw.act(Pt, Pt[:, :], psS, psS[:, :], AF.Exp, part=False)
                        Pb = pbp.next()
                        nblk[0] += 1
                        meng = "pool" if nblk[0] % 4 == 0 else "dve"
                        fw.tt(Pb, Pb[:, :], Pt, Pt[:, :], eB, eB[:, :], ALU.mult, eng=meng, part=False)
                        st[i] = (Pb, pV, pj, Vt, pK is not None)

                    def pv_stage(i):
                        t, j = blocks[i]
                        Pb, pV, pj, Vt, hasp = st.pop(i)
                        jj = j % 4
                        j0 = j - jj
                        if jj == 0:
                            grp[0] = (self.pp.next(), self.pp.next())
                        pcs = [0, 1] if hasp else [1]
                        for hh in range(2):
                            psX = grp[0][hh]
                            for ci, pc in enumerate(pcs):
                                vb = pV if pc == 0 else Vt
                                vj = pj if pc == 0 else j
                                fw.mm(lambda e, psX=psX, jj=jj, vb=vb, vj=vj, Pb=Pb, hh=hh, pc=pc, ci=ci, n=len(pcs):
                                      e.matmul(psX[:, jj * 128:(jj + 1) * 128], vb[:, vj, hh, :],
                                               Pb[:, (hh * 2 + pc) * 128:(hh * 2 + pc + 1) * 128],
                                               start=(ci == 0), stop=(ci == n - 1)),
                                      reads=[vb, Pb], writes=[psX], last=(ci == len(pcs) - 1))
                        if jj == 3:
                            cs = slice(t * 2048, (t + 1) * 2048)
                            for hh in range(2):
                                acc, psX = accH[hh], grp[0][hh]
                                av = acc[:, cs].rearrange("p (bq i r) -> p r bq i", r=d, i=128)
                                if d == 16:
                                    av = av[:, j0:j0 + 4, 0, :]
                                elif d == 4:
                                    av = av[:, j0 // 4, :, :]
                                else:
                                    av = av[:, 0, j0:j0 + 4, :]
                                pv = psX[:, :].rearrange("p (a b) -> p a b", b=128)
                                if g == _gs[0]:
                                    fw.copy("act", acc, av, psX, pv, part=True)
                                else:
                                    fw.tt(acc, av, psX, pv, acc, av, ALU.add, part=True)
                    n = len(blocks)
                    for i in range(n + LOOK):
                        if i < n:
                            qk_stage(i)
                        if i - LOOK >= 0:
                            pv_stage(i - LOOK)
                OT = S["OT"]
                for t in range(NTL):
                    for hh in range(2):
                        acc = accH[hh]
                        ln = lnp.next()
                        ob = obp.next()
                        fw.act(ln, ln[64:128, :], acc, acc[64:128, t * 2048:(t + 1) * 2048], AF.Ln, part=False)
                        for qd in range(4):
                            ps = self.pp.next()
                            c0 = t * 2048 + qd * 512
                            fw.act(ps, ps[64:128, :], ln, ln[64:128, qd * 512:(qd + 1) * 512], AF.Exp, scale=-1.0,
                                   part=False)
                            fw.tt(ob, ob[0:64, qd * 512:(qd + 1) * 512], acc, acc[0:64, c0:c0 + 512],
                                  ps, ps[64:128, :], ALU.mult, part=(qd > 0))
                        r0 = c * 128 + hh * 64
                        fw.dma(OT, OT[r0:r0 + 64, t * 2048:(t + 1) * 2048], ob, ob[0:64, :], part=True)

    def phase_C1(self, li, OTb, WO, Xin, Xout):
        fw, S, NT = self.fw, self.S, self.NT
        with fw.phase():
            self.mk_norm_pools()
            wof = fw.sb([128, 8, 1024], F32, "wof")
            fw.dma(wof, wof[:, :, :], WO, WO[:, :, :])
            wo = fw.sb([128, 8, 1024], BF16, "wo")
            fw.copy("pool", wo, wo[:, :, :], wof, wof[:, :, :], part=False)
            otp = fw.pool(2, [128, 8, 512], BF16, "ot")
            xp = fw.pool(2, [128, 8, 512], F32, "xt")
            x1p = fw.pool(2, [128, 8, 512], F32, "x1")
            h2p = fw.pool(2, [128, 8, 512], BF16, "h2")
            H2 = S["H2"]

            def load(tt):
                ot, xt = otp.next(), xp.next()
                fw.dma(ot, ot[:, :, :], OTb, self.kview(OTb, tt * 512, 512))
                fw.dma(xt, xt[:, :, :], Xin, self.kview(Xin, tt * 512, 512))
                return ot, xt
            nxt = load(0)
            for tt in range(NT // 512):
                ot, xt = nxt
                if tt + 1 < NT // 512:
                    nxt = load(tt + 1)
                x1 = x1p.next()
                for dc in range(8):
                    ps = self.pp.next()
                    for k in range(8):
                        fw.mm(lambda e, k=k, ps=ps, dc=dc, ot=ot: e.matmul(
                            ps[:, :], wo[:, k, dc * 128:(dc + 1) * 128], ot[:, k, :],
                            start=(k == 0), stop=(k == 7)), reads=[wo, ot], writes=[ps], last=(k == 7))
                    gb, gap = self.G1(li, dc)
                    fw.stt(x1, x1[:, dc, :], ps, ps[:, :], gap, xt, xt[:, dc, :], ALU.mult, ALU.add,
                           reads=[gb], part=(dc > 0))
                fw.dma(Xout, self.kview(Xout, tt * 512, 512), x1, x1[:, :, :], part=True)
                h2 = h2p.next()
                first = [True]

                def oap(k, h2=h2):
                    return h2[:, k, :]
                self.norm_mod(x1, lambda k: self.A2(li, k), lambda k: self.B2(li, k), h2, oap)
                fw.dma(H2, self.kview(H2, tt * 512, 512), h2, h2[:, :, :], part=True)

    def phase_C2(self, li, Xin, Xout):
        fw, I, S, NT = self.fw, self.I, self.S, self.NT
        H2 = S["H2"]
        ST = 1024
        with fw.phase():
            wd = fw.sb([128, 22, 1024], BF16, "wd")
            wdf = fw.pool(2, [128, 1024], F32, "wdf")
            for m in range(22):
                wf = wdf.next()
                fw.dma(wf, wf[:, :], I["w_down"], I["w_down"][li, m])
                fw.copy("pool", wd, wd[:, m, :], wf, wf[:, :], part=(m > 0))
            cw = fw.sb([128, 44, 4], F32, "cw")
            fw.dma(cw, cw[:, :, :], I["cw"], I["cw"][li])
            Hh = fw.sb([128, 44, 2], F32, "halo")
            fw.memset("pool", Hh, Hh[:, :, :], 0.0, part=False)
            h2p = fw.pool(1, [128, 8, ST], BF16, "h2T")
            mT = fw.sb([128, 22, ST], BF16, "mT")
            wufp = fw.pool(2, [128, 2, 8, 128], F32, "wuf")
            wubp = fw.pool(2, [128, 2, 8, 128], BF16, "wub")
            up = fw.pool(4, [128, 514], F32, "u")
            tp = fw.pool(6, [128, 512], F32, "t")
            sap = fw.pool(2, [128, 512], F32, "sa")
            x1p = fw.pool(4, [128, 512], F32, "x1c")
            x2p = fw.pool(4, [128, 512], F32, "x2c")
            full4 = lambda b: b[:, :, :, :]
            for st in range(NT // ST):
                h2T = h2p.next()
                fw.dma(h2T, h2T[:, :, :], H2, self.kview(H2, st * ST, ST))
                def load_wu(j):
                    wf = wufp.next()
                    fw.dma(wf, wf[:, :, :, :], I["w_up"], I["w_up"][li, j])
                    wb = wubp.next()
                    fw.copy("act", wb, wb[:, 0, :, :], wf, wf[:, 0, :, :], part=False)
                    fw.copy("act", wb, wb[:, 1, :, :], wf, wf[:, 1, :, :], part=True)
                    return wb
                nxt = load_wu(0)
                for j in range(22):
                    wu = nxt
                    if j + 1 < 22:
                        nxt = load_wu(j + 1)
                    for tt in range(ST // 512):
                        t3 = []
                        for ab in range(2):
                            ps = self.pp.next()
                            for k in range(8):
                                fw.mm(lambda e, k=k, ps=ps, ab=ab, wu=wu, tt=tt, h2T=h2T: e.matmul(
                                    ps[:, :], wu[:, ab, k, :], h2T[:, k, tt * 512:(tt + 1) * 512],
                                    start=(k == 0), stop=(k == 7)), reads=[wu, h2T], writes=[ps], last=(k == 7))
                            ch = ab * 22 + j
                            u = up.next()
                            fw.copy("pool", u, u[:, 0:2], Hh, Hh[:, ch, :], part=False)
                            fw.copy("act", u, u[:, 2:514], ps, ps[:, :], part=True)
                            fw.copy("pool", Hh, Hh[:, ch, :], u, u[:, 512:514], part=True)
                            t1 = tp.next()
                            fw.act(t1, t1[:, :], ps, ps[:, :], AF.Identity, scale=cw[:, ch, 2:3], bias=cw[:, ch, 3:4],
                                   reads=[cw], part=False)
                            t2 = tp.next()
                            fw.stt(t2, t2[:, :], u, u[:, 1:513], cw[:, ch, 1:2], t1, t1[:, :], ALU.mult, ALU.add,
                                   reads=[cw], part=False)
                            t3b = tp.next()
                            fw.stt(t3b, t3b[:, :], u, u[:, 0:512], cw[:, ch, 0:1], t2, t2[:, :], ALU.mult, ALU.add,
                                   reads=[cw], part=False)
                            t3.append(t3b)
                        sa = sap.next()
                        fw.act(sa, sa[:, :], t3[0], t3[0][:, :], AF.Silu, part=False)
                        fw.tt(mT, mT[:, j, tt * 512:(tt + 1) * 512], sa, sa[:, :], t3[1], t3[1][:, :], ALU.mult,
                              part=True)
                for tt in range(ST // 512):
                    t0 = st * ST + tt * 512
                    for dc in range(8):
                        x1 = x1p.next()
                        fw.dma(x1, x1[:, :], Xin, Xin[dc * 128:(dc + 1) * 128, t0:t0 + 512])
                        ps = self.pp.next()
                        for m in range(22):
                            fw.mm(lambda e, m=m, ps=ps, dc=dc, tt=tt: e.matmul(
                                ps[:, :], wd[:, m, dc * 128:(dc + 1) * 128], mT[:, m, tt * 512:(tt + 1) * 512],
                                start=(m == 0), stop=(m == 21)), reads=[wd, mT], writes=[ps], last=(m == 21))
                        x2 = x2p.next()
                        gb, gap = self.G2(li, dc)
                        fw.stt(x2, x2[:, :], ps, ps[:, :], gap, x1, x1[:, :], ALU.mult, ALU.add, reads=[gb],
                               part=False)
                        fw.dma(Xout, Xout[dc * 128:(dc + 1) * 128, t0:t0 + 512], x2, x2[:, :], part=True)

    def phase_A1(self):
        fw, I, S, NT = self.fw, self.I, self.S, self.NT
        Xin = S["X2"]
        with fw.phase():
            self.mk_norm_pools()
            hT = fw.sb([128, 8, 2048], BF16, "hT")
            xp = fw.pool(2, [128, 8, 512], F32, "xt")
            wfp = fw.pool(2, [128, 8, 512], F32, "wf")
            wbp = fw.pool(2, [128, 8, 512], BF16, "wb")
            stgp = fw.pool(3, [128, 2048], F32, "stgf")
            vstp = fw.pool(4, [128, 512], BF16, "vst")
            wgf = fw.sb([128, 8, 16], F32, "wgf")
            fw.dma(wgf, wgf[:, :, :], I["w_glr"], I["w_glr"][:, :, :])
            wgl = fw.sb([128, 8, 16], BF16, "wgl")
            fw.copy("pool", wgl, wgl[:, :, :], wgf, wgf[:, :, :], part=False)
            g16p = fw.pool(2, [16, 2048], BF16, "g16")
            full3 = lambda b: b[:, :, :]
            for st in range(NT // 2048):
                tcs = slice(st * 2048, (st + 1) * 2048)
                for tt in range(4):
                    xt = xp.next()
                    fw.dma(xt, xt[:, :, :], Xin, self.kview(Xin, st * 2048 + tt * 512, 512))
                    self.norm_mod(xt, lambda k: self.A1(1, k), lambda k: self.B1(1, k), hT,
                                  lambda k, tt=tt: hT[:, k, tt * 512:(tt + 1) * 512])
                g16 = g16p.next()
                for tt in range(4):
                    ps = self.pp.next()
                    for k in range(8):
                        fw.mm(lambda e, k=k, ps=ps, tt=tt: e.matmul(
                            ps[0:16, :], wgl[:, k, :], hT[:, k, tt * 512:(tt + 1) * 512],
                            start=(k == 0), stop=(k == 7)), reads=[wgl, hT], writes=[ps], last=(k == 7))
                    fw.copy("act", g16, g16[:, tt * 512:(tt + 1) * 512], ps, ps[0:16, :], part=(tt > 0))
                fw.dma(S["G1T"], S["G1T"][:, tcs], g16, g16[:, :], part=True)
                nxt = self.load_w_bf(wfp, wbp, I["w_in_b"], I["w_in_b"][0], full3)
                for blk in range(6):
                    wb = nxt
                    if blk + 1 < 6:
                        nxt = self.load_w_bf(wfp, wbp, I["w_in_b"], I["w_in_b"][blk + 1], full3)
                    if blk in (0, 1, 4, 5):
                        for fc in range(4):
                            stg = stgp.next()
                            for tt in range(4):
                                def evac(ps, stg=stg, tt=tt, blk=blk):
                                    oap = stg[:, tt * 512:(tt + 1) * 512]
                                    if blk == 0:
                                        fw.act(stg, oap, ps, ps[:, :], AF.Copy, scale=128.0 ** -0.5, part=(tt > 0))
                                    elif blk == 1:
                                        fw.copy("dve", stg, oap, ps, ps[:, :], part=(tt > 0))
                                    else:
                                        fw.act(stg, oap, ps, ps[:, :], AF.Silu, part=(tt > 0))
                                self.proj_fm(wb, fc, hT, tt, evac)
                            if blk == 0:
                                dst, r0 = S["Q1T"], fc * 128
                            elif blk == 1:
                                dst, r0 = S["K1T"], fc * 128
                            else:
                                dst, r0 = S["R1T"], (blk - 4) * 512 + fc * 128
                            fw.dma(dst, dst[r0:r0 + 128, tcs], stg, stg[:, :], part=True)
                    else:
                        half = blk - 2
                        dst = S["VB"]
                        for tb in range(16):
                            ps = self.pp.next()
                            for k in range(8):
                                fw.mm(lambda e, k=k, ps=ps, tb=tb, wb=wb: e.matmul(
                                    ps[:, :], hT[:, k, tb * 128:(tb + 1) * 128], wb[:, k, :],
                                    start=(k == 0), stop=(k == 7)),
                                    reads=[wb, hT], writes=[ps], last=(k == 7))
                            vst = vstp.next()
                            fw.copy("act" if tb % 2 else "dve", vst, vst[:, :], ps, ps[:, :], part=False)
                            tok0 = st * 2048 + tb * 128
                            fw.dma(dst, dst[tok0:tok0 + 128, half * 512:(half + 1) * 512], vst, vst[:, :], part=True)

    def phase_B1(self):
        fw, I, S, NT = self.fw, self.I, self.S, self.NT
        pp = self.pp
        with fw.phase():
            wgf = fw.sb([16, 512], F32, "wgatef")
            fw.dma(wgf, wgf[:, :], I["w_gate"], I["w_gate"][:, :])
            wg = fw.sb([16, 512], BF16, "wgate")
            fw.copy("pool", wg, wg[:, :], wgf, wgf[:, :], part=False)
            bg = fw.sb([128, 4], F32, "bg")
            fw.dma(bg, bg[:, :], I["b_gate"], I["b_gate"][:, :])
            nbg = fw.sb([128, 4], F32, "nbg")
            fw.op("dve", lambda e: e.tensor_scalar(out=nbg[:, :], in0=bg[:, :], scalar1=-1.0, scalar2=None,
                                                   op0=ALU.mult), reads=[bg], writes=[nbg])
            gn = fw.sb([128, 2], F32, "gn")
            fw.dma(gn, gn[:, :], I["gnorm"], I["gnorm"][:, :])
            maskc = fw.sb([128, 512], F32, "maskc")
            fw.dma(maskc, maskc[:, :], I["maskc"], I["maskc"][:, :])
            bmask = fw.sb([128, 128], F32, "bmask")
            fw.dma(bmask, bmask[:, :], I["bmask"], I["bmask"][:, :])
            Sf = [fw.pool(2, [128, 256], F32, f"Sf{h}") for h in range(4)]
            Sb = [fw.pool(12, [128, 256], BF16, f"Sb{h}") for h in range(4)]
            Sf_cur, Sb_cur = [], []
            for h in range(4):
                a, b = Sf[h].next(), Sb[h].next()
                fw.memset("pool", a, a[:, :], 0.0, part=False)
                fw.memset("pool", b, b[:, :], 0.0, part=False)
                Sf_cur.append(a)
                Sb_cur.append(b)
            glp = fw.pool(2, [16, 512], BF16, "glr")
            qfp = fw.pool(2, [128, 4, 512], F32, "qf")
            kfp = fw.pool(2, [128, 4, 512], F32, "kf")
            vtp = fw.pool(2, [128, 4, 1024], BF16, "vt")
            rtp = fw.pool(2, [128, 2, 512], F32, "rt")
            ogp = fw.pool(2, [128, 8, 512], BF16, "og")
            ofp = fw.pool(1, [128, 8, 512], F32, "of")
            tE = {n: fw.pool(2, [128, 512], F32, n) for n in ("te", "tsp", "tcum", "teb", "tenb", "tdiff", "ted")}
            tB = {n: fw.pool(2, [128, 512], BF16, n) for n in ("qt", "kt", "kd", "AT", "kdT")}
            sqp = fw.pool(2, [128, 2, 512], BF16, "sq2")
            stdp = fw.pool(2, [128, 512], F32, "std")
            rstdp = fw.pool(2, [128, 512], F32, "rstd")
            tmpp = fw.pool(2, [128, 512], F32, "tmp")
            G1T, Q1T, K1T, V1, R1T, OT = S["G1T"], S["Q1T"], S["K1T"], S["VB"], S["R1T"], S["OT"]
            Q1v = Q1T.t.ap().rearrange("(h p) t -> p h t", p=128)
            K1v = K1T.t.ap().rearrange("(h p) t -> p h t", p=128)
            V1v = V1.t.ap().rearrange("(b p) f -> p b f", p=128)
            R1v = R1T.t.ap().rearrange("(c p) t -> p c t", p=128)

            def load(tt):
                cs = slice(tt * 512, (tt + 1) * 512)
                gl, qf, kf, vt = glp.next(), qfp.next(), kfp.next(), vtp.next()
                fw.dma(gl, gl[:, :], G1T, G1T[:, cs])
                fw.dma(qf, qf[:, :, :], Q1T, Q1v[:, :, cs])
                fw.dma(kf, kf[:, :, :], K1T, K1v[:, :, cs])
                fw.dma(vt, vt[:, :, :], V1, V1v[:, tt * 4:(tt + 1) * 4, :])
                return gl, qf, kf, vt
            nxt = load(0)
            for tt in range(NT // 512):
                cs = slice(tt * 512, (tt + 1) * 512)
                gl, qf, kf, vt = nxt
                if tt + 1 < NT // 512:
                    nxt = load(tt + 1)
                og = ogp.next()
                of = ofp.next()
                for h in range(4):
                    psz = pp.next()
                    fw.mm(lambda e, psz=psz, h=h, gl=gl: e.matmul(psz[:, :], wg[:, h * 128:(h + 1) * 128], gl[:, :],
                                                                start=True, stop=True),
                          reads=[wg, gl], writes=[psz])
                    te = tE["te"].next()
                    fw.act(te, te[:, :], psz, psz[:, :], AF.Exp, scale=-1.0, bias=nbg[:, h:h + 1], reads=[nbg],
                           part=False)
                    tsp = tE["tsp"].next()
                    fw.act(tsp, tsp[:, :], te, te[:, :], AF.Ln, bias=1.0, part=False)
                    tcum = tE["tcum"].next()
                    fw.op("dve", lambda e, tcum=tcum, tsp=tsp: e.tensor_tensor_scan(
                        out=tcum[:, :], data0=maskc[:, :], data1=tsp[:, :], initial=0.0, op0=ALU.mult, op1=ALU.add),
                        reads=[maskc, tsp], writes=[tcum])
                    teb = tE["teb"].next()
                    fw.act(teb, teb[:, :], tcum, tcum[:, :], AF.Exp, scale=-1.0 / 16, part=False)
                    tenb = tE["tenb"].next()
                    fw.act(tenb, tenb[:, :], tcum, tcum[:, :], AF.Exp, scale=1.0 / 16, part=False)
                    tdiff = tE["tdiff"].next()
                    cv = tcum[:, :].rearrange("p (c t) -> p c t", t=64)
                    fw.tt(tdiff, tdiff[:, :].rearrange("p (c t) -> p c t", t=64), tcum,
                          cv[:, :, 63:64].to_broadcast([128, 8, 64]), tcum, cv, ALU.subtract, part=False)
                    ted = tE["ted"].next()
                    fw.act(ted, ted[:, :], tdiff, tdiff[:, :], AF.Exp, scale=-1.0 / 16, part=False)
                    qt, kt, kd = tB["qt"].next(), tB["kt"].next(), tB["kd"].next()
                    fw.tt(qt, qt[:, :], qf, qf[:, h, :], teb, teb[:, :], ALU.mult, part=False)
                    fw.tt(kt, kt[:, :], kf, kf[:, h, :], tenb, tenb[:, :], ALU.mult, part=False)
                    fw.tt(kd, kd[:, :], kf, kf[:, h, :], ted, ted[:, :], ALU.mult, part=False)
                    pst = self.pst
                    for b in range(4):
                        fw.mm(lambda e, b=b, kd=kd: e.transpose(pst[:, b * 128:(b + 1) * 128],
                                                                kd[:, b * 128:(b + 1) * 128], self.ident[:, :]),
                              reads=[kd, self.ident], writes=[pst], last=(b == 3))
                    kdT = tB["kdT"].next()
                    fw.copy("act", kdT, kdT[:, :], pst, pst[:, 0:512], part=False)
                    psA = pp.next()
                    for b in range(4):
                        bs = slice(b * 128, (b + 1) * 128)
                        fw.mm(lambda e, bs=bs, kt=kt, qt=qt, psA=psA: e.matmul(psA[:, bs], kt[:, bs], qt[:, bs],
                                                                            start=True, stop=True),
                              reads=[kt, qt], writes=[psA], last=(b == 3))
                    AT = tB["AT"].next()
                    fw.tt(AT, AT[:, :].rearrange("p (b t) -> p b t", t=128), psA,
                          psA[:, :].rearrange("p (b t) -> p b t", t=128), bmask,
                          bmask[:, :].unsqueeze(1).to_broadcast([128, 4, 128]), ALU.mult, part=False)
                    Sb_n = [Sb_cur[h]]
                    for n in range(8):
                        b, jj = divmod(n, 2)
                        rows = slice(jj * 64, jj * 64 + 64)
                        pskv = pp.next()
                        fw.mm(lambda e, pskv=pskv, rows=rows, b=b, kdT=kdT, vt=vt, h=h: e.matmul(
                            pskv[:, 0:256], kdT[rows, b * 128:(b + 1) * 128], vt[rows, b, h * 256:(h + 1) * 256],
                            start=True, stop=True), reads=[kdT, vt], writes=[pskv])
                        sf_new = Sf[h].next()
                        fw.stt(sf_new, sf_new[:, :], Sf_cur[h], Sf_cur[h][:, :], teb[:, n * 64 + 63:n * 64 + 64],
                               pskv, pskv[:, 0:256], ALU.mult, ALU.add, reads=[teb], part=False)
                        sb_new = Sb[h].next()
                        fw.copy("act", sb_new, sb_new[:, :], sf_new, sf_new[:, :], part=False)
                        Sf_cur[h] = sf_new
                        Sb_n.append(sb_new)
                    Sb_cur[h] = Sb_n[8]
                    for ec in range(2):
                        pso = pp.next()
                        for b in range(4):
                            fw.mm(lambda e, pso=pso, b=b, vt=vt, AT=AT, h=h, ec=ec: e.matmul(
                                pso[:, b * 128:(b + 1) * 128], vt[:, b, h * 256 + ec * 128:h * 256 + (ec + 1) * 128],
                                AT[:, b * 128:(b + 1) * 128], start=True, stop=False),
                                reads=[vt, AT], writes=[pso], last=False)
                            for jj in range(2):
                                n = 2 * b + jj
                                c0 = b * 128 + jj * 64
                                sbn = Sb_n[n]
                                fw.mm(lambda e, pso=pso, c0=c0, sbn=sbn, qt=qt, ec=ec, jj=jj: e.matmul(
                                    pso[:, c0:c0 + 64], sbn[:, ec * 128:(ec + 1) * 128], qt[:, c0:c0 + 64],
                                    start=False, stop=(jj == 1)),
                                    reads=[sbn, qt], writes=[pso], last=(jj == 1))
                        fw.copy("act", of, of[:, 2 * h + ec, :], pso, pso[:, :], part=True)
                    rt = rtp.next()
                    fw.dma(rt, rt[:, :, :], R1T, R1v[:, 2 * h:2 * h + 2, cs])
                    sq = sqp.next()
                    fw.act(sq, sq[:, :, :], of, of[:, 2 * h:2 * h + 2, :], AF.Square, part=False)
                    pss = pp.next()
                    for ec in range(2):
                        fw.mm(lambda e, ec=ec, sq=sq, pss=pss: e.matmul(pss[:, :], self.ones_bf[:, :], sq[:, ec, :],
                                                                        start=(ec == 0), stop=(ec == 1)),
                              reads=[sq, self.ones_bf], writes=[pss], last=(ec == 1))
                    std = stdp.next()
                    fw.act(std, std[:, :], pss, pss[:, :], AF.Ln, scale=1.0 / 256, bias=EPS, part=False)
                    rstd = rstdp.next()
                    fw.act(rstd, rstd[:, :], std, std[:, :], AF.Exp, scale=-0.5, part=False)
                    for ec in range(2):
                        tmp = tmpp.next()
                        fw.stt(tmp, tmp[:, :], of, of[:, 2 * h + ec, :], gn[:, ec:ec + 1], rstd, rstd[:, :],
                               ALU.mult, ALU.mult, reads=[gn], part=False)
                        fw.tt(og, og[:, 2 * h + ec, :], tmp, tmp[:, :], rt, rt[:, ec, :], ALU.mult,
                              part=not (h == 0 and ec == 0))
                fw.dma(OT, self.kview(OT, tt * 512, 512), og, og[:, :, :], part=True)

    def phase_F(self, Xin):
        fw, NT = self.fw, self.NT
        with fw.phase():
            xp = fw.pool(2, [128, 8, 512], F32, "xt")
            sqp = fw.pool(2, [128, 8, 512], BF16, "sq")
            stdp = fw.pool(2, [128, 512], F32, "std")
            rstdp = fw.pool(2, [128, 512], F32, "rstd")
            yp = fw.pool(2, [128, 8, 512], F32, "y")

            def load(tt):
                xt = xp.next()
                fw.dma(xt, xt[:, :, :], Xin, self.kview(Xin, tt * 512, 512))
                return xt
            nxt = load(0)
            for tt in range(NT // 512):
                xt = nxt
                if tt + 1 < NT // 512:
                    nxt = load(tt + 1)
                sq = sqp.next()
                fw.act(sq, sq[:, :, :], xt, xt[:, :, :], AF.Square, part=False)
                pss = self.pp.next()
                for k in range(8):
                    fw.mm(lambda e, k=k, sq=sq, pss=pss: e.matmul(pss[:, :], self.ones_bf[:, :], sq[:, k, :],
                                                                   start=(k == 0), stop=(k == 7)),
                          reads=[sq, self.ones_bf], writes=[pss], last=(k == 7))
                std = stdp.next()
                fw.act(std, std[:, :], pss, pss[:, :], AF.Ln, scale=1.0 / 1024, bias=EPS, part=False)
                rstd = rstdp.next()
                fw.act(rstd, rstd[:, :], std, std[:, :], AF.Exp, scale=-0.5, part=False)
                y = yp.next()
                for k in range(8):
                    fw.stt(y, y[:, k, :], xt, xt[:, k, :], self.nfin[:, k:k + 1], rstd, rstd[:, :],
                           ALU.mult, ALU.mult, reads=[self.nfin], part=(k > 0))
                fw.dma(self.OUT, self.kview(self.OUT, tt * 512, 512), y, y[:, :, :], part=True)


def _t5_bucket(dist):
    n_buckets, max_distance = 32, 2048
    max_exact = n_buckets // 2
    n = np.maximum(dist, max_exact).astype(np.float32)
    large = max_exact + (np.log(n / np.float32(max_exact)) / np.float32(math.log(max_distance / max_exact))
                         * np.float32(n_buckets - max_exact)).astype(np.int32)
    large = np.minimum(large, n_buckets - 1)
    return np.where(dist < max_exact, dist, large)


def _kp(w):
    return np.ascontiguousarray(w.reshape(8, 128, -1).transpose(1, 0, 2))


def prep_shared(inp):
    f = lambda a: np.ascontiguousarray(np.asarray(a, dtype=np.float32))
    o = {}
    w_ada = f(inp["w_ada"])
    o["w_ada"] = np.ascontiguousarray(w_ada.reshape(2, 8, 128, 6, 1024).transpose(0, 3, 2, 1, 4))
    o["b_ada"] = np.ascontiguousarray(f(inp["b_ada"]).reshape(2, 48, 128).transpose(0, 2, 1))
    o["nmix"] = np.ascontiguousarray(f(inp["norm_mix"]).reshape(2, 8, 128).transpose(0, 2, 1))
    o["nffn"] = np.ascontiguousarray(f(inp["norm_ffn"]).reshape(2, 8, 128).transpose(0, 2, 1))
    o["nfin"] = np.ascontiguousarray(f(inp["norm_final"]).reshape(8, 128).T)
    wia = f(inp["w_in_a"])[0]
    o["w_in_a"] = np.ascontiguousarray(wia.reshape(8, 128, 18, 512).transpose(2, 1, 0, 3))
    o["w_out_a"] = _kp(f(inp["w_out_a"])[0])
    rb = f(inp["rel_bias"])
    k = np.arange(128)[:, None]
    q = np.arange(128)[None, :]
    tb = np.empty((3, 8, 128, 2, 2, 128), np.float32)
    for g, (win, d) in enumerate(A_CONFIGS):
        for pc in range(2):
            steps = (q - k) + (128 if pc == 0 else 0)
            valid = (steps >= 0) & (steps <= 128)
            bucket = _t5_bucket(np.clip(steps, 0, 128) * d)
            for c in range(8):
                for hh in range(2):
                    h = 2 * c + hh
                    tb[g, c, :, hh, pc, :] = np.where(valid, rb[bucket, g * 16 + h], np.float32(-30000.0))
    o["tbias"] = tb.reshape(3, 8, 128, 512)
    wib = f(inp["w_in_b"])[0]
    blocks = [wib[:, 0:512], wib[:, 512:1024], wib[:, 1024:1536], wib[:, 1536:2048],
              wib[:, 2064:2576], wib[:, 2576:3088]]
    o["w_in_b"] = np.stack([_kp(b) for b in blocks])
    o["w_glr"] = _kp(wib[:, 2048:2064])
    o["w_gate"] = f(inp["w_gate_b"])[0]
    o["b_gate"] = np.ascontiguousarray(f(inp["b_gate_b"])[0].reshape(4, 128).T)
    o["gnorm"] = np.ascontiguousarray(f(inp["gnorm_b"])[0].reshape(2, 128).T)
    o["w_out_b"] = _kp(f(inp["w_out_b"])[0])
    wup = f(inp["w_up"])
    o["w_up"] = np.ascontiguousarray(wup.reshape(2, 8, 128, 2, 22, 128).transpose(0, 4, 2, 3, 1, 5))
    cwt = f(inp["conv_w"])
    cb = f(inp["conv_b"])
    cw = np.concatenate([cwt, cb[:, None, :]], axis=1)
    o["cw"] = np.ascontiguousarray(cw.reshape(2, 4, 44, 128).transpose(0, 3, 2, 1))
    o["w_down"] = np.ascontiguousarray(f(inp["w_down"]).reshape(2, 22, 128, 1024))
    mc = np.ones((128, 512), np.float32)
    mc[:, 0::64] = 0.0
    o["maskc"] = mc
    s = np.arange(128)[:, None]
    t = np.arange(128)[None, :]
    o["bmask"] = ((s // 64 == t // 64) & (s <= t)).astype(np.float32)
    o["identf"] = np.eye(128, dtype=np.float32)
    return o


def prep_core(inp, b, NT):
    x = np.asarray(inp["x"], dtype=np.float32)[b, :NT]
    c = np.asarray(inp["c"], dtype=np.float32)[b]
    return {"xT": np.ascontiguousarray(x.T), "cT": np.ascontiguousarray(c.reshape(8, 128).T)}


_PROG_CACHE = {}


def get_prog(NT, dbg=(), upto=99):
    key = (NT, tuple(dbg), upto)
    if key not in _PROG_CACHE:
        _PROG_CACHE[key] = Prog(NT, dbg, upto)
    return _PROG_CACHE[key]


def kernel(**inputs):
    NT = 8192
    prog = get_prog(NT)
    shared = prep_shared(inputs)
    in_maps = []
    real = {0: 0, 1: 1, 4: 2, 5: 3}
    zeros = None
    for core in range(8):
        if core in real:
            m = dict(shared)
            m.update(prep_core(inputs, real[core], NT))
        else:
            if zeros is None:
                zeros = {k: np.zeros_like(v) for k, v in shared.items()}
                zeros.update({k: np.zeros_like(v) for k, v in prep_core(inputs, 0, NT).items()})
            m = zeros
        in_maps.append(m)
    res = run_bass_kernel_spmd(prog.nc, in_maps, core_ids=list(range(8)))
    out = np.empty((4, NT, 1024), np.float32)
    for core, b in real.items():
        out[b] = res.results[core]["yT"].T
    return out
```
